# Optimizing a Trainium2 kernel written in Bass

```python
import math
import jax, jax.numpy as jnp
from jax import lax
import numpy as np

D_MODEL = 2048
BATCH = 16
SEQ = 256
DEPTH = 2
DEC_BATCH = 4
DEC_SEQ = 2048
PAST_LEN = 256

GRID_W = 64
N_MIXERS = 2
N_CONV = (DEPTH + 1) // 2
N_RET = DEPTH // 2
CONV_WIDTH = 31
D_FF = 4 * D_MODEL
RET_HEADS = 8
RET_DK = D_MODEL // RET_HEADS
RET_DV = 2 * D_MODEL // RET_HEADS
RET_CHUNK = 128
ROPE_BASE = 10000.0
LN_EPS = 1e-5
GN_EPS = 1e-5
ALPHA = (2.0 * DEPTH) ** 0.25
BETA = (8.0 * DEPTH) ** -0.25

kernel_name = "hybrid_conv_retention_diffusion_step"


def layer_norm(x, g, b):
    xf = x.astype(jnp.float32)
    mu = jnp.mean(xf, -1, keepdims=True)
    var = jnp.mean(jnp.square(xf - mu), -1, keepdims=True)
    y = (xf - mu) * lax.rsqrt(var + LN_EPS)
    return (y * g.astype(jnp.float32) + b.astype(jnp.float32)).astype(x.dtype)


def conv_module(h, w_pw1, b_pw1, w_dw, b_dw, cn_g, cn_b, w_pw2, b_pw2):
    u = h @ w_pw1 + b_pw1
    a, gate = jnp.split(u, 2, axis=-1)
    u = a * jax.nn.sigmoid(gate)
    u = lax.conv_general_dilated(
        u, w_dw[:, None, :], window_strides=(1,),
        padding=((CONV_WIDTH // 2, CONV_WIDTH // 2),),
        dimension_numbers=('NWC', 'WIO', 'NWC'),
        feature_group_count=D_MODEL) + b_dw
    u = jax.nn.silu(layer_norm(u, cn_g, cn_b))
    return u @ w_pw2 + b_pw2


def rope_2d(x):
    t = x.shape[2]
    rows = t // GRID_W
    row = jnp.repeat(jnp.arange(rows, dtype=jnp.float32), GRID_W)
    col = jnp.tile(jnp.arange(GRID_W, dtype=jnp.float32), rows)
    quarter = RET_DK // 4
    inv = ROPE_BASE ** (-jnp.arange(quarter, dtype=jnp.float32) / quarter)
    ang = jnp.stack([row[:, None] * inv, col[:, None] * inv], axis=1)
    cos, sin = jnp.cos(ang), jnp.sin(ang)
    xr = x.reshape(x.shape[:3] + (2, 2, quarter))
    x1, x2 = xr[..., 0, :], xr[..., 1, :]
    out = jnp.stack([x1 * cos - x2 * sin, x2 * cos + x1 * sin], axis=-2)
    return out.reshape(x.shape)


def retention_scan(q, k, v, log_gamma, s0):
    b, h, t, _ = q.shape
    n = t // RET_CHUNK
    idx = jnp.arange(RET_CHUNK, dtype=jnp.float32)
    lg = log_gamma[:, None]
    decay_in = jnp.exp(lg * (idx + 1.0))[None, :, :, None]
    decay_out = jnp.exp(lg * (RET_CHUNK - 1.0 - idx))[None, :, :, None]
    decay_chunk = jnp.exp(log_gamma * RET_CHUNK)[None, :, None, None]
    diff = idx[:, None] - idx[None, :]
    mask = jnp.where(diff >= 0, jnp.exp(lg[..., None] * jnp.maximum(diff, 0.0)), 0.0)

    def chunks(a):
        return jnp.moveaxis(a.reshape(b, h, n, RET_CHUNK, a.shape[-1]), 2, 0)

    def step(s, qkv):
        qc, kc, vc = qkv
        scores = jnp.einsum('bhid,bhjd->bhij', qc, kc) * mask
        inner = jnp.einsum('bhij,bhje->bhie', scores, vc)
        cross = jnp.einsum('bhid,bhde->bhie', qc, s) * decay_in
        s_new = decay_chunk * s + jnp.einsum('bhjd,bhje->bhde', kc * decay_out, vc)
        return s_new, inner + cross

    s_final, out = lax.scan(step, s0, (chunks(q), chunks(k), chunks(v)))
    out = jnp.moveaxis(out, 0, 2).reshape(b, h, t, RET_DV)
    return out, s_final


def retention(h, s0, latent, w_in, log2_rate, w_o):
    b, t, _ = h.shape
    proj = h @ w_in
    q, k, v, g = jnp.split(proj, [D_MODEL, 2 * D_MODEL, 4 * D_MODEL], axis=-1)

    def heads(a, d):
        return a.reshape(b, t, RET_HEADS, d).transpose(0, 2, 1, 3).astype(jnp.float32)

    q = heads(q, RET_DK) * (RET_DK ** -0.5)
    k = heads(k, RET_DK)
    v = heads(v, RET_DV)
    if latent:
        q = rope_2d(q)
        k = rope_2d(k)
    log_gamma = jnp.log1p(-jnp.exp2(log2_rate.astype(jnp.float32)))
    s0 = s0.astype(jnp.float32)
    out_f, s_f = retention_scan(q, k, v, log_gamma[0], s0[:, 0])
    flip = lambda a: jnp.flip(a, axis=2)
    out_b, s_b = retention_scan(flip(q), flip(k), flip(v), log_gamma[1], s0[:, 1])
    out = out_f + flip(out_b)
    mu = jnp.mean(out, -1, keepdims=True)
    var = jnp.mean(jnp.square(out - mu), -1, keepdims=True)
    out = (out - mu) * lax.rsqrt(var + GN_EPS)
    out = out.transpose(0, 2, 1, 3).reshape(b, t, RET_HEADS * RET_DV).astype(h.dtype)
    y = (jax.nn.silu(g) * out) @ w_o
    return y, jnp.stack([s_f, s_b], axis=1).astype(h.dtype)


def run_trunk(x, cond, init_states, latent, w_mod, b_mod, ln1_g, ln1_b, ln2_g, ln2_b,
              w_pw1, b_pw1, w_dw, b_dw, cn_g, cn_b, w_pw2, b_pw2,
              w_ret_in, ret_log2_rate, w_ret_o, w_ff1, b_ff1, w_ff2, b_ff2):
    b = x.shape[0]
    states = []
    for i in range(DEPTH):
        mod = (jax.nn.silu(cond) @ w_mod[i] + b_mod[i])[:, None, :]
        sh1, sc1, g1, sh2, sc2, g2 = jnp.split(mod, 6, axis=-1)
        hm = x * (1 + sc1) + sh1
        j = i // N_MIXERS
        if i % N_MIXERS == 0:
            y = conv_module(hm, w_pw1[j], b_pw1[j], w_dw[j], b_dw[j], cn_g[j], cn_b[j],
                            w_pw2[j], b_pw2[j])
        else:
            if init_states is None:
                s0 = jnp.zeros((b, 2, RET_HEADS, RET_DK, RET_DV), jnp.float32)
            else:
                s0 = init_states[:, j]
            y, s = retention(hm, s0, latent, w_ret_in[j], ret_log2_rate[j], w_ret_o[j])
            states.append(s)
        x = layer_norm(ALPHA * x + g1 * y, ln1_g[i], ln1_b[i])
        hm = x * (1 + sc2) + sh2
        y = jnp.square(jax.nn.relu(hm @ w_ff1[i] + b_ff1[i])) @ w_ff2[i] + b_ff2[i]
        x = layer_norm(ALPHA * x + g2 * y, ln2_g[i], ln2_b[i])
    return x, jnp.stack(states, axis=1)


def setup_inputs(seed: int = 0) -> dict:
    key = jax.random.key(seed)
    ks = jax.random.split(key, 32)
    nrm = lambda k, shape, s: jax.random.normal(k, shape, jnp.float32) * s
    D = D_MODEL
    rate = -(5.0 + jnp.arange(RET_HEADS, dtype=jnp.float32))
    ret_log2_rate = jnp.broadcast_to(rate, (N_RET, 2, RET_HEADS)) + nrm(ks[20], (N_RET, 2, RET_HEADS), 0.1)
    return {
        "x_prompt": nrm(ks[0], (BATCH, SEQ, D), 1.0),
        "x_sample": nrm(ks[1], (DEC_BATCH, DEC_SEQ, D), 1.0),
        "state_ret": nrm(ks[2], (DEC_BATCH, N_RET, 2, RET_HEADS, RET_DK, RET_DV), 1.0),
        "c": nrm(ks[3], (DEC_BATCH, D), 1.0),
        "c_ctx": nrm(ks[4], (D,), 1.0),
        "w_mod": nrm(ks[5], (DEPTH, D, 6 * D), 0.5 * D ** -0.5),
        "b_mod": nrm(ks[6], (DEPTH, 6 * D), 0.02),
        "ln1_g": 1.0 + nrm(ks[7], (DEPTH, D), 0.02),
        "ln1_b": nrm(ks[8], (DEPTH, D), 0.02),
        "ln2_g": 1.0 + nrm(ks[9], (DEPTH, D), 0.02),
        "ln2_b": nrm(ks[10], (DEPTH, D), 0.02),
        "w_pw1": nrm(ks[11], (N_CONV, D, 2 * D), D ** -0.5),
        "b_pw1": nrm(ks[12], (N_CONV, 2 * D), 0.02),
        "w_dw": nrm(ks[13], (N_CONV, CONV_WIDTH, D), CONV_WIDTH ** -0.5),
        "b_dw": nrm(ks[14], (N_CONV, D), 0.02),
        "cn_g": 1.0 + nrm(ks[15], (N_CONV, D), 0.02),
        "cn_b": nrm(ks[16], (N_CONV, D), 0.02),
        "w_pw2": nrm(ks[17], (N_CONV, D, D), BETA * D ** -0.5),
        "b_pw2": nrm(ks[18], (N_CONV, D), 0.02),
        "w_ret_in": nrm(ks[19], (N_RET, D, 6 * D), D ** -0.5),
        "ret_log2_rate": ret_log2_rate,
        "w_ret_o": nrm(ks[21], (N_RET, 2 * D, D), BETA * (2 * D) ** -0.5),
        "w_ff1": nrm(ks[22], (DEPTH, D, D_FF), D ** -0.5),
        "b_ff1": nrm(ks[23], (DEPTH, D_FF), 0.02),
        "w_ff2": nrm(ks[24], (DEPTH, D_FF, D), BETA * D_FF ** -0.5),
        "b_ff2": nrm(ks[25], (DEPTH, D), 0.02),
    }


def reference(x_prompt, x_sample, state_ret, c, c_ctx, w_mod, b_mod, ln1_g, ln1_b, ln2_g, ln2_b,
              w_pw1, b_pw1, w_dw, b_dw, cn_g, cn_b, w_pw2, b_pw2,
              w_ret_in, ret_log2_rate, w_ret_o, w_ff1, b_ff1, w_ff2, b_ff2):
    weights = (w_mod, b_mod, ln1_g, ln1_b, ln2_g, ln2_b,
               w_pw1, b_pw1, w_dw, b_dw, cn_g, cn_b, w_pw2, b_pw2,
               w_ret_in, ret_log2_rate, w_ret_o, w_ff1, b_ff1, w_ff2, b_ff2)
    cond_ctx = jnp.broadcast_to(c_ctx, (x_prompt.shape[0], D_MODEL))
    y_prompt, new_state_ret = run_trunk(x_prompt, cond_ctx, None, False, *weights)
    y_sample, _ = run_trunk(x_sample, c, state_ret, True, *weights)
    return (y_prompt, y_sample, new_state_ret)
```

```python
import math
from contextlib import ExitStack

import numpy as np
import concourse.bass as bass
import concourse.mybir as mybir
from concourse.bass_utils import run_bass_kernel_spmd

F32 = mybir.dt.float32
BF16 = mybir.dt.bfloat16
AF = mybir.ActivationFunctionType
ALU = mybir.AluOpType
AX = mybir.AxisListType

D = 2048
KC = 16
DFF = 8192
NH = 8
DK = 256
DV = 512
CH = 128
CONVW = 31
HALO = 15
ALPHA = (2.0 * 2) ** 0.25
LN_EPS = 1e-5
TS = 1024
TP = 512
NS = 4
ENG = ["pe", "act", "dve", "pool", "sp"]

VOFF = {}
_o = 0
for _n, _c in [("b_mod", 2 * 96), ("ln1_g", 32), ("ln1_b", 32), ("ln2_g", 32), ("ln2_b", 32),
               ("b_pw1", 32), ("w_dw", CONVW * 16), ("b_dw", 16), ("cn_g", 16), ("cn_b", 16),
               ("b_pw2", 16), ("b_ff1", 128), ("b_ff2", 32)]:
    VOFF[_n] = _o
    _o += _c
NV = _o

COFF = {}
_o = 0
for _n, _c in [("ident", 128), ("perm", 128), ("Pm", 128), ("Nm", 128), ("indF", 128), ("indB", 128),
               ("posC1", 1024), ("posL1", 1024), ("cos0", 1024), ("sin0", 1024), ("cos1", 1024),
               ("sin1", 1024), ("pcol", 2), ("flags", 4), ("ones", 128)]:
    COFF[_n] = _o
    _o += _c
NCST = _o


class Buf:
    __slots__ = ("name", "w", "r", "excl")

    def __init__(self, name, excl=False):
        self.name = name
        self.w = None
        self.r = {}
        self.excl = excl


class _Rec:
    def __init__(self):
        self.calls = []

    def __getattr__(self, name):
        def f(*a, **kw):
            self.calls.append((name, a, kw))
            return None
        return f


def _record(fn):
    r = _Rec()
    fn(r)
    return r.calls


class Tracker:
    def __init__(self, nc, es, n_dma_sems=64):
        self.nc = nc
        self.items = {e: [] for e in ENG}
        self.cnt = {e: 0 for e in ENG}
        self.pending = {e: False for e in ENG}
        self.sem = {e: es.enter_context(nc.semaphore("c_" + e)) for e in ENG}
        self.waited = {e: {} for e in ENG}
        self.dsem = [es.enter_context(nc.semaphore("d%d" % i)) for i in range(n_dma_sems)]
        self.dval = [0] * n_dma_sems
        self.dnext = 0

    def new_dma_sem(self):
        i = self.dnext
        self.dnext += 1
        assert i < len(self.dsem)
        return i

    def _wait(self, eng, key, val):
        if self.waited[eng].get(key, 0) >= val:
            return
        self.waited[eng][key] = val
        self.items[eng].append(("wait", key, val))

    def _deps(self, eng, reads, writes):
        for b in reads:
            if b.w is not None:
                k, v = b.w
                if k == eng and eng == "pe":
                    continue
                self._wait(eng, k, v)
            if b.excl:
                for k, v in b.r.items():
                    if k != eng:
                        self._wait(eng, k, v)
        for b in writes:
            if b.w is not None:
                k, v = b.w
                if not (k == eng and eng == "pe"):
                    self._wait(eng, k, v)
            for k, v in b.r.items():
                if not (k == eng and eng == "pe"):
                    self._wait(eng, k, v)

    def _mark(self, ev, reads, writes):
        k, v = ev
        for b in reads:
            if b.r.get(k, 0) < v:
                b.r[k] = v
        for b in writes:
            b.w = ev
            b.r = {}

    def op(self, eng, fn, reads=(), writes=(), signal=True):
        self._deps(eng, reads, writes)
        if signal:
            self.cnt[eng] += 1
            ev = (eng, self.cnt[eng])
            self.pending[eng] = False
        else:
            ev = (eng, self.cnt[eng] + 1)
            self.pending[eng] = True
        calls = _record(fn)
        assert len(calls) == 1
        self.items[eng].append(("op", calls[0], signal))
        self._mark(ev, reads, writes)

    def dma(self, eng, fn, semi, reads=(), writes=(), n=1):
        self._deps(eng, reads, writes)
        key = ("d", semi)
        if self.dval[semi] > 0:
            self._wait(eng, key, self.dval[semi])
        calls = _record(fn)
        assert len(calls) == n, (len(calls), n)
        self.dval[semi] += 16 * n
        ev = (key, self.dval[semi])
        self.items[eng].append(("dma", calls, semi))
        self._mark(ev, reads, writes)

    def barrier(self):
        for e in ENG:
            assert not self.pending[e], e
        for e in ENG:
            for k in ENG:
                if k != e and self.cnt[k] > 0:
                    self._wait(e, k, self.cnt[k])
            for i in range(self.dnext):
                if self.dval[i] > 0:
                    self._wait(e, ("d", i), self.dval[i])

    def replay(self, eng, e):
        for it in self.items[eng]:
            if it[0] == "wait":
                key, val = it[1], it[2]
                sem = self.sem[key] if isinstance(key, str) else self.dsem[key[1]]
                e.wait_ge(sem, val)
            elif it[0] == "op":
                name, a, kw = it[1]
                ins = getattr(e, name)(*a, **kw)
                if it[2]:
                    ins.then_inc(self.sem[eng], 1)
            else:
                for (name, a, kw) in it[1]:
                    getattr(e, name)(*a, **kw).then_inc(self.dsem[it[2]], 16)


class Cfg:
    pass


class StopBuild(Exception):
    pass


def build_program(stop=None, dbg=None, n_cores=8):
    nc = bass.Bass("TRN2", target_bir_lowering=False)
    es = ExitStack()
    dbg_d = nc.dram_tensor("dbg", [128, 4096], F32, kind="ExternalOutput").ap() if stop is not None else None

    def din(name, shape, dt=F32):
        return nc.dram_tensor(name, list(shape), dt, kind="ExternalInput").ap()

    def dout(name, shape, dt=F32):
        return nc.dram_tensor(name, list(shape), dt, kind="ExternalOutput").ap()

    xs_d = din("xs", [D, TS + 2 * HALO])
    xp_d = din("xp", [D, TP])
    s0_d = din("s0", [2, NH, DK, DV])
    condT_d = din("condT", [128, KC * 2])
    vecT_d = din("vecT", [128, NV])
    cst_d = din("cst", [128, NCST])
    rate_d = din("rate", [128, 16])
    w_mod_d = din("w_mod", [2, D, 6 * D])
    w_pw1_d = din("w_pw1", [D, 2 * D])
    w_pw2_d = din("w_pw2", [D, D])
    w_in_d = din("w_ret_in", [D, 6 * D])
    w_o_d = din("w_ret_o", [2 * D, D])
    w_ff1_d = din("w_ff1", [2, D, DFF])
    w_ff2_d = din("w_ff2", [2, DFF, D])
    ys_d = dout("ys", [D, TS])
    yp_d = dout("yp", [D, TP])
    st_d = dout("st", [2, 2, NH, DK, DV])

    xspill = nc.dram_tensor("xspill", [128, KC * TS], F32)
    olsp = nc.dram_tensor("olsp", [NH, 128, 8 * DV], F32)
    qsp = nc.dram_tensor("qsp", [NH, 128, 2 * TS], BF16)
    gtsp = nc.dram_tensor("gtsp", [NH, 128, 4 * TS], BF16)
    exin = [nc.dram_tensor("exin%d" % h, [2 * 256, DV], F32) for h in range(NH)]
    exout = [nc.dram_tensor("exout%d" % h, [2 * 2 * 256, DV], F32) for h in range(NH)]

    def sb(name, shape, dt):
        return es.enter_context(nc.sbuf_tensor(name, list(shape), dt))

    xT = sb("xT", [128, KC, TS], F32)
    hmT = sb("hmT", [128, KC, TS + 32], BF16)
    bufB = sb("bufB", [128, KC, TS], BF16)
    ring = sb("ring", [128, NS, 4096], BF16)
    vecT = sb("vecT_s", [128, NV], F32)
    cst = sb("cst_s", [128, 8], F32)
    modT = sb("modT", [128, 2, 2, 96], F32)
    mder = sb("mder", [128, 2, 2, 64], F32)
    identb = sb("identb", [128, 128], BF16)
    permb = sb("permb", [128, 128], BF16)
    onesb = sb("onesb", [128, 128], BF16)
    condb = sb("condb", [128, KC * 2], BF16)
    condf = sb("condf", [128, KC * 2], F32)
    halo = sb("halo", [128, KC, 2 * HALO], F32)
    rstdv = sb("rstdv", [128, TS], F32)
    cvv = sb("cvv", [128, TS], F32)
    tmpA = sb("tmpA", [128, TS], F32)
    tmpB = sb("tmpB", [128, TS], F32)
    tmpC = sb("tmpC", [128, 512], F32)
    tmpD = sb("tmpD", [128, 512], F32)
    tb16a = sb("tb16a", [128, 512], BF16)
    tb16b = sb("tb16b", [128, 512], BF16)
    ratet = sb("ratet", [128, 16], F32)
    LG = sb("LG", [128, 16], F32)
    NLG = sb("NLG", [128, 16], F32)
    DCH = sb("DCH", [128, 16], F32)
    DOUT = sb("DOUT", [128, 16], F32)
    LB1025 = sb("LB1025", [128, 16], F32)
    LB129 = sb("LB129", [128, 16], F32)
    maskfb = sb("maskfb", [128, NH, 128], F32)
    small = sb("small", [128, 64], F32)

    psum = [es.enter_context(nc.psum_tensor("ps%d" % i, [128, 512], F32)) for i in range(8)]

    T = Tracker(nc, es)

    B_x = [Buf("x%d" % k) for k in range(KC)]
    B_hm = [Buf("hm%d" % k) for k in range(KC)]
    B_bb = [Buf("bb%d" % k) for k in range(KC)]
    B_ring = [Buf("ring%d" % i) for i in range(NS)]
    B_ps = [Buf("ps%d" % i, excl=True) for i in range(8)]
    B_misc = {}

    def mb(name):
        if name not in B_misc:
            B_misc[name] = Buf(name)
        return B_misc[name]

    ring_sem = [T.new_dma_sem() for _ in range(NS)]
    st = {"ring_n": 0, "ps_n": 0}

    def next_ps():
        i = st["ps_n"] % 7
        st["ps_n"] += 1
        return i

    CSMALL = {"pcol": 0, "flags": 2}
    cview = {}

    def cs(name, n=None, off=0):
        if name in CSMALL:
            o = CSMALL[name] + off
            return cst[:, o:o + (n if n is not None else 1)]
        base = cview[name]
        return base[:, off:off + (n if n is not None else 1)]

    def vs(name, idx):
        o = VOFF[name] + idx
        return vecT[:, o:o + 1]

    ring_busy = [False] * NS
    cur_stream = [None]
    mp = {"n": 0}

    def pump_reserve():
        return 1 if 0 < mp["n"] < 96 else 0

    def wload(src_ap):
        kdim = KC
        if isinstance(src_ap, tuple):
            src_ap, kdim = src_ap
        i = None
        for j in range(NS):
            cand = (st["ring_n"] + j) % NS
            if not ring_busy[cand]:
                i = cand
                break
        assert i is not None, "no free weight ring slot"
        st["ring_n"] = i + 1
        ring_busy[i] = True
        dst = ring[:, i, :].rearrange("p (k n) -> p k n", k=kdim)
        T.dma("pool", lambda e, d=dst, s=src_ap: [e.dma_start(out=d, in_=s)], ring_sem[i],
              reads=(), writes=(B_ring[i],))
        return i

    def wview(i):
        return ring[:, i, :].rearrange("p (k n) -> p k n", k=KC)

    class WStream:
        def __init__(self, srcs, hold=2, lookahead=NS - 2):
            if cur_stream[0] is not None:
                cur_stream[0].close()
            cur_stream[0] = self
            self.srcs = srcs
            self.hold = hold
            self.lookahead = lookahead
            self.issued = 0
            self.released = 0
            self.slots = []
            self.pos = -1
            self._fill()

        def _fill(self):
            while (self.issued < len(self.srcs) and self.issued - (self.pos + 1) < self.lookahead
                   and sum(ring_busy) < NS - pump_reserve()):
                self.slots.append(wload(self.srcs[self.issued]))
                self.issued += 1

        def _release_upto(self, j):
            while self.released <= j and self.released < self.issued:
                ring_busy[self.slots[self.released]] = False
                self.released += 1

        def get(self, j):
            self.pos = max(self.pos, j)
            self._release_upto(j - self.hold)
            if self.issued <= j:
                assert not all(ring_busy), "weight ring exhausted"
                while self.issued <= j:
                    self.slots.append(wload(self.srcs[self.issued]))
                    self.issued += 1
            self._fill()
            return self.slots[j]

        def close(self):
            self._release_upto(len(self.srcs))

    def wtile_w(w2d, k0, c0):
        return (w2d[k0:k0 + 1024, c0:c0 + 512].rearrange("(k p) n -> p k n", p=128), 8)

    def wview_w(i):
        return ring[:, i, :].rearrange("p (k n) -> p k n", k=8)

    def wtile(w2d, k0, c0):
        return w2d[k0:k0 + D, c0:c0 + 256].rearrange("(k p) n -> p k n", p=128)

    def mm_group(ps_i, ps_ap, pairs, reads):
        n = len(pairs)
        for j, (l, r) in enumerate(pairs):
            T.op("pe", lambda e, o=ps_ap, l=l, r=r, a=(j == 0), z=(j == n - 1): e.matmul(o, l, r, start=a, stop=z),
                 reads=reads if j == 0 else (), writes=(B_ps[ps_i],), signal=(j == n - 1))

    sem_c = T.new_dma_sem()
    xflat = xT[:, :, :].rearrange("p k t -> p (k t)")
    bflat = bufB[:, :, :].rearrange("p k t -> p (k t)")
    cstA = xflat[:, 0:896]
    _o = 0
    for _n in ("ident", "perm", "Pm", "Nm", "indF", "indB", "ones"):
        cview[_n] = cstA[:, _o:_o + 128]
        _o += 128
    T.dma("sp", lambda e: [e.dma_start(out=vecT[:, :], in_=vecT_d[:, :]),
                           e.dma_start(out=cst[:, 0:6], in_=cst_d[:, COFF["pcol"]:COFF["pcol"] + 6]),
                           e.dma_start(out=cstA[:, 0:768], in_=cst_d[:, COFF["ident"]:COFF["ident"] + 768]),
                           e.dma_start(out=cstA[:, 768:896], in_=cst_d[:, COFF["ones"]:COFF["ones"] + 128]),
                           e.dma_start(out=condf[:, :], in_=condT_d[:, :]),
                           e.dma_start(out=ratet[:, :], in_=rate_d[:, :])], sem_c,
          writes=(mb("vec"), mb("cst"), mb("condf"), mb("rate")), n=6)
    T.op("dve", lambda e: e.tensor_copy(identb[:, :], cs("ident", 128)), reads=(mb("cst"),), writes=(mb("identb"),))
    T.op("dve", lambda e: e.tensor_copy(permb[:, :], cs("perm", 128)), reads=(mb("cst"),), writes=(mb("permb"),))
    T.op("dve", lambda e: e.tensor_copy(onesb[:, :], cs("ones", 128)), reads=(mb("cst"),), writes=(mb("onesb"),))
    T.op("act", lambda e: e.activation(condb[:, :], condf[:, :], AF.Silu), reads=(mb("condf"),), writes=(mb("condb"),))

    T.op("act", lambda e: e.activation(small[:, 0:16], ratet[:, :], AF.Exp, scale=math.log(2.0)),
         reads=(mb("rate"),), writes=(mb("small"),))
    xx = small[:, 0:16]
    pp = small[:, 16:32]
    T.op("dve", lambda e: e.tensor_scalar(pp, xx, 0.2, 0.25, ALU.mult, ALU.add), reads=(mb("small"),), writes=(mb("small2"),))
    for cc in (1.0 / 3.0, 0.5, 1.0):
        T.op("dve", lambda e: e.tensor_tensor(pp, pp, xx, ALU.mult), reads=(mb("small2"), mb("small")), writes=(mb("small2"),))
        T.op("dve", lambda e, cc=cc: e.tensor_scalar(pp, pp, cc, None, ALU.add), reads=(mb("small2"),), writes=(mb("small2"),))
    T.op("dve", lambda e: e.tensor_tensor(NLG[:, :], pp, xx, ALU.mult), reads=(mb("small2"), mb("small")), writes=(mb("NLG"),))
    T.op("dve", lambda e: e.tensor_scalar(LG[:, :], NLG[:, :], -1.0, None, ALU.mult), reads=(mb("NLG"),), writes=(mb("LG"),))
    T.op("dve", lambda e: e.tensor_scalar(LB1025[:, :], LG[:, :], float(TS + 1), None, ALU.mult), reads=(mb("LG"),), writes=(mb("LB1025"),))
    T.op("dve", lambda e: e.tensor_scalar(LB129[:, :], LG[:, :], float(CH + 1), None, ALU.mult), reads=(mb("LG"),), writes=(mb("LB129"),))
    T.op("act", lambda e: e.activation(DCH[:, :], LG[:, :], AF.Exp, scale=float(CH)), reads=(mb("LG"),), writes=(mb("DCH"),))
    T.op("dve", lambda e: e.tensor_scalar(small[:, 32:40], LG[:, 0:8], cs("pcol", 1, 0), None, ALU.mult),
         reads=(mb("LG"), mb("cst")), writes=(mb("small3"),))
    T.op("dve", lambda e: e.tensor_scalar(small[:, 40:48], LG[:, 8:16], cs("pcol", 1, 1), None, ALU.mult),
         reads=(mb("LG"), mb("cst")), writes=(mb("small3"),))
    T.op("act", lambda e: e.activation(DOUT[:, :], small[:, 32:48], AF.Exp), reads=(mb("small3"),), writes=(mb("DOUT"),))
    for h in range(NH):
        T.op("act", lambda e, h=h: e.activation(tmpC[:, 0:128], cs("Pm", 128), AF.Exp, scale=LG[:, h:h + 1]),
             reads=(mb("LG"), mb("cst")), writes=(mb("tmpC"),))
        T.op("act", lambda e, h=h: e.activation(tmpD[:, 0:128], cs("Nm", 128), AF.Exp, scale=LG[:, 8 + h:9 + h]),
             reads=(mb("LG"), mb("cst")), writes=(mb("tmpD"),))
        T.op("dve", lambda e: e.tensor_tensor(tmpC[:, 0:128], tmpC[:, 0:128], cs("indF", 128), ALU.mult),
             reads=(mb("tmpC"),), writes=(mb("tmpC"),))
        T.op("dve", lambda e: e.tensor_tensor(tmpD[:, 0:128], tmpD[:, 0:128], cs("indB", 128), ALU.mult),
             reads=(mb("tmpD"),), writes=(mb("tmpD"),))
        T.op("dve", lambda e, h=h: e.tensor_tensor(maskfb[:, h, :], tmpC[:, 0:128], tmpD[:, 0:128], ALU.add),
             reads=(mb("tmpC"), mb("tmpD")), writes=(mb("maskfb"),))

    MODB = 7
    mod_tiles = [(l, cb) for l in range(2) for cb in range(48)]

    def mod_evac(l, m0, m1):
        for c in range(2):
            src = psum[MODB][:, l * 192:(l + 1) * 192].rearrange("p (m c) -> p c m", c=2)[:, c, m0:m1]
            T.op("dve", lambda e: e.tensor_tensor(
                modT[:, l, c, m0:m1], src, vecT[:, VOFF["b_mod"] + l * 96 + m0:VOFF["b_mod"] + l * 96 + m1], ALU.add),
                reads=(B_ps[MODB], mb("vec")), writes=(mb("modT"),))
        for c in range(2):
            if m0 <= 16 and m1 >= 32:
                T.op("dve", lambda e: e.tensor_scalar(mder[:, l, c, 0:16], modT[:, l, c, 16:32], 1.0, None, ALU.add),
                     reads=(mb("modT"),), writes=(mb("mder"),))
            if m0 <= 64 and m1 >= 80:
                T.op("dve", lambda e: e.tensor_scalar(mder[:, l, c, 16:32], modT[:, l, c, 64:80], 1.0, None, ALU.add),
                     reads=(mb("modT"),), writes=(mb("mder"),))

    pump_slot = [None]

    def _pump_prefetch():
        if mp["n"] < len(mod_tiles) and pump_slot[0] is None:
            l, cb = mod_tiles[mp["n"]]
            pump_slot[0] = wload(wtile(w_mod_d[l], 0, cb * 256))

    def mod_pump(k):
        for _ in range(k):
            if mp["n"] >= len(mod_tiles):
                return
            _pump_prefetch()
            l, cb = mod_tiles[mp["n"]]
            si = pump_slot[0]
            wv = wview(si)
            for m2 in range(2):
                mchunk = cb * 2 + m2
                pairs = [(wv[:, k_, m2 * 128:(m2 + 1) * 128], condb[:, 2 * k_:2 * k_ + 2]) for k_ in range(KC)]
                col = l * 192 + 2 * mchunk
                mm_group(MODB, psum[MODB][:, col:col + 2], pairs, reads=(B_ring[si], mb("condb")))
            ring_busy[si] = False
            pump_slot[0] = None
            mp["n"] += 1
            _pump_prefetch()
            if cb == 15 and l == 0:
                mod_evac(0, 0, 32)
            if cb == 47:
                mod_evac(l, 32, 96) if l == 0 else mod_evac(l, 0, 96)

    def mod_bulk(k):
        srcs0 = [wtile(w_mod_d[mod_tiles[i][0]], 0, mod_tiles[i][1] * 256) for i in range(k)]
        ws0 = WStream(srcs0, hold=1, lookahead=NS - 1)
        for i in range(k):
            l, cb = mod_tiles[i]
            si = ws0.get(i)
            wv = wview(si)
            for m2 in range(2):
                mchunk = cb * 2 + m2
                pairs = [(wv[:, k_, m2 * 128:(m2 + 1) * 128], condb[:, 2 * k_:2 * k_ + 2]) for k_ in range(KC)]
                col = l * 192 + 2 * mchunk
                mm_group(MODB, psum[MODB][:, col:col + 2], pairs, reads=(B_ring[si], mb("condb")))
        ws0.close()
        cur_stream[0] = None
        mp["n"] = k
        mod_evac(0, 0, 32)

    mod_bulk(16)
    T.barrier()
    _stop_now = (stop == "prologue")

    def checkpoint(name):
        if stop == name:
            raise StopBuild()

    def tiles_of(cfg):
        return [(t0, min(512, cfg.T - t0)) for t0 in range(0, cfg.T, 512)]

    def mod_apply(cfg, scp_fn, sh_fn):
        for k in range(KC):
            if k % 2 == 0:
                T.op("act", lambda e, k=k: e.activation(hmT[:, k, 0:cfg.T], xT[:, k, 0:cfg.T], AF.Identity,
                                                        bias=sh_fn(k), scale=scp_fn(k)),
                     reads=(B_x[k], mb("mder"), mb("modT")), writes=(B_hm[k],))
            else:
                T.op("dve", lambda e, k=k: e.tensor_scalar(hmT[:, k, 0:cfg.T], xT[:, k, 0:cfg.T], scp_fn(k), sh_fn(k),
                                                           ALU.mult, ALU.add),
                     reads=(B_x[k], mb("mder"), mb("modT")), writes=(B_hm[k],))

    def prescale_x(cfg, gvec_fn, bias_name, bias_off):
        for k in range(KC):
            if bias_name is None:
                T.op("dve", lambda e, k=k: e.tensor_scalar(xT[:, k, 0:cfg.T], xT[:, k, 0:cfg.T], ALPHA, None, ALU.mult),
                     reads=(B_x[k],), writes=(B_x[k],))
            else:
                T.op("dve", lambda e, k=k: e.tensor_tensor(small[:, 48 + (k % 8):49 + (k % 8)], gvec_fn(k),
                                                           vs(bias_name, bias_off + k), ALU.mult),
                     reads=(mb("modT"), mb("vec")), writes=(mb("small4_%d" % (k % 8)),))
                T.op("dve", lambda e, k=k: e.tensor_scalar(xT[:, k, 0:cfg.T], xT[:, k, 0:cfg.T], ALPHA,
                                                           small[:, 48 + (k % 8):49 + (k % 8)], ALU.mult, ALU.add),
                     reads=(B_x[k], mb("small4_%d" % (k % 8))), writes=(B_x[k],))

    def dense_acc_into_x(cfg, srcs, in_tile_fn, in_bufs, gvec_fn, nk=KC):
        nkh = nk // KC
        ws = WStream(srcs, hold=nkh, lookahead=NS - 1)
        tl = tiles_of(cfg)
        for mbk in range(8):
            slots = [ws.get(mbk * nkh + kh) for kh in range(nkh)]
            mod_pump(1)
            for m2 in range(2):
                m = mbk * 2 + m2
                for (t0, tn) in tl:
                    pi = next_ps()
                    pairs = []
                    for kh in range(nkh):
                        wv = wview(slots[kh])
                        for k in range(KC):
                            pairs.append((wv[:, k, m2 * 128:(m2 + 1) * 128], in_tile_fn(kh * KC + k, t0, tn)))
                    mm_group(pi, psum[pi][:, 0:tn], pairs, reads=tuple(B_ring[s] for s in slots) + tuple(in_bufs))
                    T.op("dve", lambda e, pi=pi, m=m, t0=t0, tn=tn: e.scalar_tensor_tensor(
                        xT[:, m, t0:t0 + tn], psum[pi][:, 0:tn], gvec_fn(m), xT[:, m, t0:t0 + tn], ALU.mult, ALU.add),
                        reads=(B_ps[pi], B_x[m], mb("modT")), writes=(B_x[m],))

    def layer_norm(cfg, g_name, b_name, voff, hm_fold):
        tl = tiles_of(cfg)
        for (t0, tn) in tl:
            p1 = next_ps()
            p2 = next_ps()
            for k in range(KC):
                ta = tb16a if k % 2 == 0 else tb16b
                tq = tmpC if k % 2 == 0 else tmpD
                na = "tb16a" if k % 2 == 0 else "tb16b"
                nq = "tmpC" if k % 2 == 0 else "tmpD"
                T.op("dve", lambda e, k=k, ta=ta: e.tensor_copy(ta[:, 0:tn], xT[:, k, t0:t0 + tn]),
                     reads=(B_x[k],), writes=(mb(na),))
                T.op("pe", lambda e, k=k, ta=ta: e.matmul(psum[p1][:, 0:tn], onesb[:, :], ta[:, 0:tn], start=(k == 0), stop=(k == KC - 1)),
                     reads=(mb(na), mb("onesb")), writes=(B_ps[p1],), signal=True)
                tq16 = tq[:, 0:256].bitcast(BF16)
                T.op("act", lambda e, k=k, tq16=tq16: e.activation(tq16[:, 0:tn], xT[:, k, t0:t0 + tn], AF.Square),
                     reads=(B_x[k],), writes=(mb(nq),))
                T.op("pe", lambda e, k=k, tq16=tq16: e.matmul(psum[p2][:, 0:tn], onesb[:, :], tq16[:, 0:tn], start=(k == 0), stop=(k == KC - 1)),
                     reads=(mb(nq), mb("onesb")), writes=(B_ps[p2],), signal=True)
            ln_finish_stats(p1, p2, t0, tn, D)
        ln_apply(cfg, g_name, b_name, voff, hm_fold)

    def ln_finish_stats(p1, p2, t0, tn, dd):
        mean = tmpA[:, t0:t0 + tn]
        msq = tmpB[:, t0:t0 + tn]
        T.op("dve", lambda e: e.tensor_scalar(mean, psum[p1][:, 0:tn], 1.0 / dd, None, ALU.mult),
             reads=(B_ps[p1],), writes=(mb("tmpA"),))
        T.op("dve", lambda e: e.tensor_tensor(msq, mean, mean, ALU.mult), reads=(mb("tmpA"),), writes=(mb("tmpB"),))
        T.op("dve", lambda e: e.scalar_tensor_tensor(msq, psum[p2][:, 0:tn], 1.0 / dd, msq, ALU.mult, ALU.subtract),
             reads=(B_ps[p2], mb("tmpB")), writes=(mb("tmpB"),))
        T.op("dve", lambda e: e.tensor_scalar(msq, msq, LN_EPS, None, ALU.add), reads=(mb("tmpB"),), writes=(mb("tmpB"),))
        T.op("act", lambda e: e.activation(msq, msq, AF.Sqrt), reads=(mb("tmpB"),), writes=(mb("tmpB"),))
        T.op("dve", lambda e: e.reciprocal(rstdv[:, t0:t0 + tn], msq),
             reads=(mb("tmpB"),), writes=(mb("rstdv"),))
        T.op("dve", lambda e: e.scalar_tensor_tensor(cvv[:, t0:t0 + tn], mean, -1.0, rstdv[:, t0:t0 + tn], ALU.mult, ALU.mult),
             reads=(mb("tmpA"), mb("rstdv")), writes=(mb("cvv"),))

    def ln_apply(cfg, g_name, b_name, voff, hm_fold):
        Tn = cfg.T
        if hm_fold is not None:
            l, c = hm_fold
            G2 = mder[:, l, c, 32:48]
            B2 = mder[:, l, c, 48:64]
            gv = vecT[:, VOFF[g_name] + voff:VOFF[g_name] + voff + 16]
            bv = vecT[:, VOFF[b_name] + voff:VOFF[b_name] + voff + 16]
            T.op("dve", lambda e: e.tensor_tensor(G2, gv, mder[:, l, c, 16:32], ALU.mult),
                 reads=(mb("vec"), mb("mder")), writes=(mb("mderG"),))
            T.op("dve", lambda e: e.tensor_tensor(B2, bv, mder[:, l, c, 16:32], ALU.mult),
                 reads=(mb("vec"), mb("mder")), writes=(mb("mderB"),))
            T.op("dve", lambda e: e.tensor_tensor(B2, B2, modT[:, l, c, 48:64], ALU.add),
                 reads=(mb("mderB"), mb("modT")), writes=(mb("mderB"),))
        for k in range(KC):
            T.op("dve", lambda e, k=k: e.tensor_tensor(xT[:, k, 0:Tn], xT[:, k, 0:Tn], rstdv[:, 0:Tn], ALU.mult),
                 reads=(B_x[k], mb("rstdv")), writes=(B_x[k],))
            T.op("dve", lambda e, k=k: e.tensor_tensor(xT[:, k, 0:Tn], xT[:, k, 0:Tn], cvv[:, 0:Tn], ALU.add),
                 reads=(B_x[k], mb("cvv")), writes=(B_x[k],))
            if hm_fold is not None:
                T.op("act", lambda e, k=k: e.activation(hmT[:, k, 0:Tn], xT[:, k, 0:Tn], AF.Identity,
                                                        bias=B2[:, k:k + 1], scale=G2[:, k:k + 1]),
                     reads=(B_x[k], mb("mderG"), mb("mderB")), writes=(B_hm[k],))
            T.op("act", lambda e, k=k: e.activation(xT[:, k, 0:Tn], xT[:, k, 0:Tn], AF.Identity,
                                                    bias=vs(b_name, voff + k), scale=vs(g_name, voff + k)),
                 reads=(B_x[k], mb("vec")), writes=(B_x[k],))

    def ffn(cfg, l):
        c = cfg.cond
        prescale_x(cfg, lambda k: modT[:, l, c, 80 + k:81 + k], "b_ff2", l * 16)
        tl = tiles_of(cfg)
        for q in range(4):
            srcs = []
            for b in range(4):
                srcs += [wtile_w(w_ff1_d[l], 0, q * 2048 + b * 512), wtile_w(w_ff1_d[l], 1024, q * 2048 + b * 512)]
            ws = WStream(srcs, hold=2, lookahead=2)
            for b in range(4):
                ws._release_upto(2 * b - 1)
                sA = ws.get(2 * b)
                sB = ws.get(2 * b + 1)
                wA, wB = wview_w(sA), wview_w(sB)
                mod_pump(2)
                for h4 in range(4):
                    hc = b * 4 + h4
                    for (t0, tn) in tl:
                        pi = next_ps()
                        pairs = [(wA[:, k, h4 * 128:(h4 + 1) * 128], hmT[:, k, t0:t0 + tn]) for k in range(8)]
                        pairs += [(wB[:, k, h4 * 128:(h4 + 1) * 128], hmT[:, 8 + k, t0:t0 + tn]) for k in range(8)]
                        mm_group(pi, psum[pi][:, 0:tn], pairs, reads=(B_ring[sA], B_ring[sB]) + tuple(B_hm))
                        tq = tmpC if (hc + t0 // 512) % 2 == 0 else tmpD
                        nm = "tmpC" if (hc + t0 // 512) % 2 == 0 else "tmpD"
                        T.op("act", lambda e: e.activation(
                            tq[:, 0:tn], psum[pi][:, 0:tn], AF.Relu, bias=vs("b_ff1", l * 64 + q * 16 + hc)),
                            reads=(B_ps[pi], mb("vec")), writes=(mb(nm),))
                        T.op("dve", lambda e: e.tensor_tensor(
                            bufB[:, hc, t0:t0 + tn], tq[:, 0:tn], tq[:, 0:tn], ALU.mult),
                            reads=(mb(nm),), writes=(B_bb[hc],))
            srcs2 = []
            for b in range(4):
                srcs2 += [wtile_w(w_ff2_d[l], q * 2048, b * 512), wtile_w(w_ff2_d[l], q * 2048 + 1024, b * 512)]
            ws2 = WStream(srcs2, hold=2, lookahead=2)
            for b in range(4):
                ws2._release_upto(2 * b - 1)
                sA = ws2.get(2 * b)
                sB = ws2.get(2 * b + 1)
                wA, wB = wview_w(sA), wview_w(sB)
                mod_pump(2)
                for m4 in range(4):
                    m = b * 4 + m4
                    for (t0, tn) in tl:
                        pi = next_ps()
                        pairs = [(wA[:, k, m4 * 128:(m4 + 1) * 128], bufB[:, k, t0:t0 + tn]) for k in range(8)]
                        pairs += [(wB[:, k, m4 * 128:(m4 + 1) * 128], bufB[:, 8 + k, t0:t0 + tn]) for k in range(8)]
                        mm_group(pi, psum[pi][:, 0:tn], pairs, reads=(B_ring[sA], B_ring[sB]) + tuple(B_bb))
                        T.op("dve", lambda e: e.scalar_tensor_tensor(
                            xT[:, m, t0:t0 + tn], psum[pi][:, 0:tn], modT[:, l, c, 80 + m:81 + m], xT[:, m, t0:t0 + tn], ALU.mult, ALU.add),
                            reads=(B_ps[pi], B_x[m], mb("modT")), writes=(B_x[m],))

    def conv_mixer(cfg):
        c = cfg.cond
        Tn = cfg.T
        nseg = len(cfg.segs)
        L = cfg.segs[0][1]
        LP = L + 2 * HALO
        tl = tiles_of(cfg)
        with ExitStack() as ss:
            upad = [ss.enter_context(nc.sbuf_tensor("upad%d_%d" % (i, Tn), [128, nseg * LP], BF16)) for i in range(2)]
            NDG = 8
            dg = ss.enter_context(nc.sbuf_tensor("dg_%d" % Tn, [128, NDG, 128], BF16))
            B_dg = [Buf("dg%d" % i) for i in range(NDG)]
            sg = [tmpC, tmpD]
            B_up = [Buf("upad0"), Buf("upad1")]
            B_sg = [mb("tmpC"), mb("tmpD")]
            if not cfg.halo:
                for i in range(2):
                    T.op("dve", lambda e, i=i: e.memset(upad[i][:, :], 0.0), writes=(B_up[i],))
            srcs = []
            for i in range(8):
                srcs.append(wtile(w_pw1_d, 0, 256 * i))
                srcs.append(wtile(w_pw1_d, 0, 2048 + 256 * i))
            ws = WStream(srcs)
            stt = {"sgn": 0, "dgn": 0, "slots": None}

            def pw1_chunk(ch):
                i, c2 = ch // 2, ch % 2
                if c2 == 0:
                    stt["slots"] = (ws.get(2 * i), ws.get(2 * i + 1))
                sa, sgi = stt["slots"]
                up = upad[ch % 2]
                bup = B_up[ch % 2]
                up3 = up[:, :].rearrange("p (s l) -> p s l", s=nseg)
                groups = [(t0, tn, None) for (t0, tn) in tl]
                if cfg.halo:
                    groups.append((TS, 2 * HALO, "halo"))
                for (t0, tn, kind) in groups:
                    pa = next_ps()
                    pg = next_ps()
                    wa = wview(sa)
                    wg = wview(sgi)
                    pairs_a = [(wa[:, k, c2 * 128:(c2 + 1) * 128], hmT[:, k, t0:t0 + tn]) for k in range(KC)]
                    pairs_g = [(wg[:, k, c2 * 128:(c2 + 1) * 128], hmT[:, k, t0:t0 + tn]) for k in range(KC)]
                    mm_group(pa, psum[pa][:, 0:tn], pairs_a, reads=(B_ring[sa],) + tuple(B_hm) + (mb("hmhalo"),))
                    mm_group(pg, psum[pg][:, 0:tn], pairs_g, reads=(B_ring[sgi],) + tuple(B_hm) + (mb("hmhalo"),))
                    s_i = stt["sgn"] % 2
                    stt["sgn"] += 1
                    T.op("act", lambda e: e.activation(
                        sg[s_i][:, 0:tn], psum[pg][:, 0:tn], AF.Sigmoid, bias=vs("b_pw1", 16 + ch)),
                        reads=(B_ps[pg], mb("vec")), writes=(B_sg[s_i],))
                    if kind == "halo":
                        for side in range(2):
                            dst = up[:, 0:HALO] if side == 0 else up[:, HALO + L:2 * HALO + L]
                            T.op("dve", lambda e: e.scalar_tensor_tensor(
                                tmpA[:, side * HALO:(side + 1) * HALO], psum[pa][:, side * HALO:(side + 1) * HALO], vs("b_pw1", ch),
                                sg[s_i][:, side * HALO:(side + 1) * HALO], ALU.add, ALU.mult),
                                reads=(B_ps[pa], B_sg[s_i], mb("vec")), writes=(mb("tmpA"),))
                            T.op("dve", lambda e: e.tensor_scalar(
                                dst, tmpA[:, side * HALO:(side + 1) * HALO], cs("flags", 1, 1 if side == 0 else 0), None, ALU.mult),
                                reads=(mb("tmpA"), mb("cst")), writes=(bup,))
                    else:
                        if nseg == 1:
                            dst = up[:, HALO + t0:HALO + t0 + tn]
                            src_a = psum[pa][:, 0:tn]
                            src_s = sg[s_i][:, 0:tn]
                        else:
                            dst = up3[:, :, HALO:HALO + L]
                            src_a = psum[pa][:, 0:tn].rearrange("p (s l) -> p s l", s=nseg)
                            src_s = sg[s_i][:, 0:tn].rearrange("p (s l) -> p s l", s=nseg)
                        T.op("dve", lambda e: e.scalar_tensor_tensor(
                            dst, src_a, vs("b_pw1", ch), src_s, ALU.add, ALU.mult),
                            reads=(B_ps[pa], B_sg[s_i], mb("vec")), writes=(bup,))
                mod_pump(2)

            def conv_chunk(ch):
                up = upad[ch % 2]
                bup = B_up[ch % 2]
                up3 = up[:, :].rearrange("p (s l) -> p s l", s=nseg)
                outs = []
                if nseg == 1:
                    for (t0, tn) in tl:
                        outs.append((next_ps(), tn, (lambda k, t0=t0, tn=tn: up[:, t0 + k:t0 + k + tn]), bufB[:, ch, t0:t0 + tn]))
                else:
                    for sI in range(nseg):
                        outs.append((next_ps(), L, (lambda k, sI=sI: up3[:, sI, k:k + L]), bufB[:, ch, sI * L:(sI + 1) * L]))
                for k in range(CONVW):
                    di = stt["dgn"] % NDG
                    stt["dgn"] += 1
                    T.op("dve", lambda e: e.tensor_scalar(dg[:, di, :], identb[:, :], vs("w_dw", k * 16 + ch), None, ALU.mult),
                         reads=(mb("identb"), mb("vec")), writes=(B_dg[di],))
                    for oi, (pb, n_, rfn, dst) in enumerate(outs):
                        T.op("pe", lambda e: e.matmul(psum[pb][:, 0:n_], dg[:, di, :], rfn(k), start=(k == 0), stop=(k == CONVW - 1)),
                             reads=(B_dg[di], bup), writes=(B_ps[pb],), signal=(k == CONVW - 1 or oi == len(outs) - 1))
                for oi, (pb, n_, rfn, dst) in enumerate(outs):
                    if oi % 2 == 0:
                        T.op("act", lambda e: e.activation(dst, psum[pb][:, 0:n_], AF.Identity, bias=vs("b_dw", ch)),
                             reads=(B_ps[pb], mb("vec")), writes=(B_bb[ch],))
                    else:
                        T.op("dve", lambda e: e.tensor_scalar(dst, psum[pb][:, 0:n_], vs("b_dw", ch), None, ALU.add),
                             reads=(B_ps[pb], mb("vec")), writes=(B_bb[ch],))

            pw1_chunk(0)
            for ch in range(16):
                if ch + 1 < 16:
                    pw1_chunk(ch + 1)
                conv_chunk(ch)
            for (t0, tn) in tl:
                p1 = next_ps()
                p2 = next_ps()
                for k in range(KC):
                    tq = tmpC if k % 2 == 0 else tmpD
                    nq = "tmpC" if k % 2 == 0 else "tmpD"
                    T.op("pe", lambda e, k=k: e.matmul(psum[p1][:, 0:tn], onesb[:, :], bufB[:, k, t0:t0 + tn], start=(k == 0), stop=(k == KC - 1)),
                         reads=(B_bb[k], mb("onesb")), writes=(B_ps[p1],), signal=True)
                    tq16 = tq[:, 0:256].bitcast(BF16)
                    T.op("act", lambda e, k=k, tq16=tq16: e.activation(tq16[:, 0:tn], bufB[:, k, t0:t0 + tn], AF.Square),
                         reads=(B_bb[k],), writes=(mb(nq),))
                    T.op("pe", lambda e, k=k, tq16=tq16: e.matmul(psum[p2][:, 0:tn], onesb[:, :], tq16[:, 0:tn], start=(k == 0), stop=(k == KC - 1)),
                         reads=(mb(nq), mb("onesb")), writes=(B_ps[p2],), signal=True)
                ln_finish_stats(p1, p2, t0, tn, D)
            for k in range(KC):
                ta = tmpA if k % 2 == 0 else tmpB
                bta = mb("tmpA" if k % 2 == 0 else "tmpB")
                T.op("dve", lambda e, k=k, ta=ta: e.tensor_tensor(ta[:, 0:Tn], bufB[:, k, 0:Tn], rstdv[:, 0:Tn], ALU.mult),
                     reads=(B_bb[k], mb("rstdv")), writes=(bta,))
                T.op("dve", lambda e, k=k, ta=ta: e.tensor_tensor(ta[:, 0:Tn], ta[:, 0:Tn], cvv[:, 0:Tn], ALU.add),
                     reads=(bta, mb("cvv")), writes=(bta,))
                T.op("act", lambda e, k=k, ta=ta: e.activation(hmT[:, k, 0:Tn], ta[:, 0:Tn], AF.Silu,
                                                               bias=vs("cn_b", k), scale=vs("cn_g", k)),
                     reads=(bta, mb("vec")), writes=(B_hm[k],))
            prescale_x(cfg, lambda k: modT[:, 0, c, 32 + k:33 + k], "b_pw2", 0)
            srcs2 = [wtile(w_pw2_d, 0, mbk * 256) for mbk in range(8)]
            dense_acc_into_x(cfg, srcs2, lambda k, t0, tn: hmT[:, k, t0:t0 + tn], B_hm,
                             lambda m: modT[:, 0, c, 32 + m:33 + m])

    def retention(cfg):
        c = cfg.cond
        Tn = cfg.T
        NT = Tn // CH
        tl = tiles_of(cfg)
        nseg = len(cfg.segs)
        cps = cfg.segs[0][1] // CH
        sem_sp = T.new_dma_sem()
        sem_sp2 = T.new_dma_sem()
        sem_q = [T.new_dma_sem() for _ in range(2)]
        sem_ol = [T.new_dma_sem() for _ in range(2)]
        sem_s0 = T.new_dma_sem()
        sem_st = [T.new_dma_sem() for _ in range(2)]
        T.barrier()
        def exchange_head(hh):
            T.op("pool", lambda e: e.collective_compute(
                "AllGather", ALU.bypass, replica_groups=[[2 * i, 2 * i + 1] for i in range(n_cores // 2)],
                ins=[exin[hh].ap().opt()], outs=[exout[hh].ap().opt()]), reads=(mb("exin%d" % hh),), writes=(mb("exout%d" % hh),))

        if True:
            qT = bufB[:, 0:2, :]
            kT = bufB[:, 2:4, :]
            qf = bufB[:, 4:6, :]
            qb = bufB[:, 6:8, :]
            vtok = bufB[:, 8:12, :].rearrange("p a (b e) -> p (a b) e", e=DV)
            kfb = [bflat[:, 12288 + i * 512:12288 + (i + 1) * 512].rearrange("p (d k) -> p d k", d=2) for i in range(2)]
            sTm = [bflat[:, 13312 + i * 128:13312 + (i + 1) * 128] for i in range(2)]
            qraw = bflat[:, 13568:14080]
            oloc1 = xflat[:, 0:4096].rearrange("p (n e) -> p n e", e=DV)
            oloc = [oloc1, oloc1]
            ropeT = xflat[:, 4096:8192]
            posC = xflat[:, 8192:9216]
            Sst = xflat[:, 9216:11264].rearrange("p (d c e) -> p d c e", d=2, c=2)
            rtmp = [xflat[:, 11264 + i * 512:11264 + (i + 1) * 512] for i in range(2)]
            sbst2 = xflat[:, 12288:16384].bitcast(BF16).rearrange("p (n e) -> p n e", e=DV)
            Sbf2 = [bflat[:, 14080 + i * 1024:14080 + (i + 1) * 1024].rearrange("p (c e) -> p c e", c=2) for i in range(2)]
            tabf = tmpA
            tabb = tmpB
            cview["posC1"] = posC
            for i_, n_ in enumerate(("cos0", "sin0", "cos1", "sin1")):
                cview[n_] = ropeT[:, i_ * 1024:(i_ + 1) * 1024]
            sem_tb = T.new_dma_sem()
            T.dma("sp", lambda e: [e.dma_start(out=ropeT, in_=cst_d[:, COFF["cos0"]:COFF["cos0"] + 4096]),
                                   e.dma_start(out=posC, in_=cst_d[:, COFF["posC1"]:COFF["posC1"] + 1024])],
                  sem_tb, writes=(mb("cst"),), n=2)
            checkpoint("rA0")
            SB_all = [mb("S0_0"), mb("S0_1"), mb("S1_0"), mb("S1_1")]
            _bol = Buf("ol")
            B_ol = [_bol, _bol]
            B_kfb = [Buf("kfb0"), Buf("kfb1")]
            B_sT = [Buf("sT0"), Buf("sT1")]
            B_rt = [Buf("rt0"), Buf("rt1")]
            rt_n = [0]

            for h in range(NH):
                par = h % 2
                if h == 0:
                    srcsA = []
                    for hh in range(NH):
                        srcsA += [wtile(w_in_d, 0, hh * 256), wtile(w_in_d, 0, 2048 + hh * 256),
                                  wtile(w_in_d, 0, 4096 + hh * 512), wtile(w_in_d, 0, 4096 + hh * 512 + 256)]
                    wsA = WStream(srcsA, hold=2, lookahead=2)
                for which in range(2):
                    si = wsA.get(4 * h + which)
                    wv = wview(si)
                    dstT = qT if which == 0 else kT
                    bd = mb("qT" if which == 0 else "kT")
                    for dc in range(2):
                        for (t0, tn) in tl:
                            pi = next_ps()
                            pairs = [(wv[:, k, dc * 128:(dc + 1) * 128], hmT[:, k, t0:t0 + tn]) for k in range(KC)]
                            mm_group(pi, psum[pi][:, 0:tn], pairs, reads=(B_ring[si],) + tuple(B_hm))
                            scale = (DK ** -0.5) if which == 0 else 1.0
                            if not cfg.rope:
                                T.op("act", lambda e, pi=pi, dstT=dstT, dc=dc, t0=t0, tn=tn, scale=scale: e.activation(
                                    dstT[:, dc, t0:t0 + tn], psum[pi][:, 0:tn], AF.Identity, scale=scale),
                                    reads=(B_ps[pi],), writes=(bd,))
                            else:
                                T.op("act", lambda e, pi=pi, tn=tn, scale=scale: e.activation(
                                    qraw[:, 0:tn], psum[pi][:, 0:tn], AF.Identity, scale=scale),
                                    reads=(B_ps[pi],), writes=(mb("qraw"),))
                                p2 = next_ps()
                                mm_group(p2, psum[p2][:, 0:tn], [(permb[:, :], qraw[:, 0:tn])], reads=(mb("qraw"), mb("permb")))
                                r0 = rt_n[0] % 2
                                rt_n[0] += 1
                                cosn = "cos%d" % dc
                                sinn = "sin%d" % dc
                                T.op("dve", lambda e, pi=pi, r0=r0, t0=t0, tn=tn, cosn=cosn, scale=scale: e.scalar_tensor_tensor(
                                    rtmp[r0][:, 0:tn], psum[pi][:, 0:tn], scale, cs(cosn, tn, t0), ALU.mult, ALU.mult),
                                    reads=(B_ps[pi], mb("cst")), writes=(B_rt[r0],))
                                T.op("dve", lambda e, p2=p2, t0=t0, tn=tn, sinn=sinn: e.tensor_tensor(
                                    tmpC[:, 0:tn], psum[p2][:, 0:tn], cs(sinn, tn, t0), ALU.mult),
                                    reads=(B_ps[p2], mb("cst")), writes=(mb("tmpC"),))
                                T.op("dve", lambda e, r0=r0, dstT=dstT, dc=dc, t0=t0, tn=tn: e.tensor_tensor(
                                    dstT[:, dc, t0:t0 + tn], rtmp[r0][:, 0:tn], tmpC[:, 0:tn], ALU.add),
                                    reads=(B_rt[r0], mb("tmpC")), writes=(bd,))
                if h == 0:
                    checkpoint("rA1")
                T.dma("sp", lambda e, h=h: [e.dma_start(out=qsp[h, :, 0:2 * Tn].rearrange("p (a t) -> p a t", a=2),
                                                        in_=qT[:, :, 0:Tn])],
                      sem_q[par], reads=(mb("qT"),), writes=(mb("qsp%d" % h),))
                if h == 0:
                    checkpoint("rA1b")
                sv0 = wsA.get(4 * h + 2)
                sv1 = wsA.get(4 * h + 3)
                for n in range(NT):
                    pi = next_ps()
                    for half, si in enumerate((sv0, sv1)):
                        wv = wview(si)
                        pairs = [(hmT[:, k, n * CH:(n + 1) * CH], wv[:, k, :]) for k in range(KC)]
                        mm_group(pi, psum[pi][:, half * 256:(half + 1) * 256], pairs, reads=(B_ring[si],) + tuple(B_hm))
                    if n % 2 == 0:
                        T.op("act", lambda e, pi=pi, n=n: e.activation(vtok[:, n, :], psum[pi][:, :], AF.Copy),
                             reads=(B_ps[pi],), writes=(mb("vtok"),))
                    else:
                        T.op("dve", lambda e, pi=pi, n=n: e.tensor_copy(vtok[:, n, :], psum[pi][:, :]),
                             reads=(B_ps[pi],), writes=(mb("vtok"),))
                if h == 0:
                    checkpoint("rA2")
                if cfg.exchange and h > 0:
                    exchange_head(h - 1)
                T.op("act", lambda e, h=h: e.activation(tabf[:, 0:Tn], cs("posC1", Tn), AF.Exp, scale=LG[:, h:h + 1]),
                     reads=(mb("LG"), mb("cst")), writes=(mb("tabf"),))
                T.op("act", lambda e, h=h: e.activation(tabb[:, 0:Tn], cs("posC1", Tn), AF.Exp, scale=NLG[:, 8 + h:9 + h],
                                                        bias=LB129[:, 8 + h:9 + h]),
                     reads=(mb("NLG"), mb("LB129"), mb("cst")), writes=(mb("tabb"),))
                for dc in range(2):
                    T.op("dve", lambda e, dc=dc: e.tensor_tensor(qf[:, dc, 0:Tn], qT[:, dc, 0:Tn], tabf[:, 0:Tn], ALU.mult),
                         reads=(mb("qT"), mb("tabf")), writes=(mb("qf"),))
                    T.op("dve", lambda e, dc=dc: e.tensor_tensor(qb[:, dc, 0:Tn], qT[:, dc, 0:Tn], tabb[:, 0:Tn], ALU.mult),
                         reads=(mb("qT"), mb("tabb")), writes=(mb("qb"),))

                if h == 0:
                    checkpoint("rA3")

                def ktok(n, d, kslot):
                    pk = next_ps()
                    pk16 = psum[pk][:, 0:128].bitcast(BF16)
                    for dc in range(2):
                        T.op("pe", lambda e, dc=dc, pk16=pk16, n=n: e.transpose(
                            pk16[:, dc * 128:(dc + 1) * 128], kT[:, dc, n * CH:(n + 1) * CH], identb[:, :]),
                            reads=(mb("kT"), mb("identb")), writes=(B_ps[pk],), signal=(dc == 1))
                    T.op("dve", lambda e, pk16=pk16, d=d, kslot=kslot, h=h: e.tensor_scalar(
                        kfb[kslot][:, d, :], pk16[:, :], DOUT[:, d * 8 + h:d * 8 + h + 1], None, ALU.mult),
                        reads=(B_ps[pk], mb("DOUT")), writes=(B_kfb[kslot],))

                def ds_mm(n, d, kslot):
                    banks = []
                    for dc in range(2):
                        pi = next_ps()
                        mm_group(pi, psum[pi][:, :], [(kfb[kslot][:, d, dc * 128:(dc + 1) * 128], vtok[:, n, :])],
                                 reads=(B_kfb[kslot], mb("vtok")))
                        banks.append(pi)
                    return banks

                def s_update(d, banks):
                    for dc, pi in enumerate(banks):
                        T.op("dve", lambda e: e.scalar_tensor_tensor(
                            Sst[:, d, dc, :], Sst[:, d, dc, :], DCH[:, d * 8 + h:d * 8 + h + 1], psum[pi][:, :], ALU.mult, ALU.add),
                            reads=(B_ps[pi], mb("S%d_%d" % (d, dc)), mb("DCH")), writes=(mb("S%d_%d" % (d, dc)),))

                kn = [0]
                for s in range(nseg):
                    base = s * cps
                    if cfg.init_state:
                        T.dma("sp", lambda e, h=h: [e.dma_start(out=Sst[:, d, :, :], in_=s0_d[d, h].rearrange("(c p) e -> p c e", p=128))
                                                    for d in range(2)], sem_s0, writes=tuple(SB_all), n=2)
                        for d in range(2):
                            T.op("dve", lambda e, d=d: e.tensor_scalar(Sst[:, d, :, :], Sst[:, d, :, :], cs("flags", 1, d), None, ALU.mult),
                                 reads=(mb("S%d_0" % d), mb("S%d_1" % d), mb("cst")), writes=(mb("S%d_0" % d), mb("S%d_1" % d)))
                    else:
                        for d in range(2):
                            T.op("dve", lambda e, d=d: e.memset(Sst[:, d, :, :], 0.0), writes=(mb("S%d_0" % d), mb("S%d_1" % d)))
                    order1 = [base + n for n in range(cps - 1, -1, -1)]
                    ktok(order1[0], 1, kn[0] % 2)
                    for i1, gn in enumerate(order1):
                        ks = kn[0] % 2
                        kn[0] += 1
                        if i1 + 1 < len(order1):
                            ktok(order1[i1 + 1], 1, kn[0] % 2)
                        banks = ds_mm(gn, 1, ks)
                        for dc in range(2):
                            T.op("act", lambda e: e.activation(sbst2[:, 2 * gn + dc, :], Sst[:, 1, dc, :], AF.Copy),
                                 reads=(mb("S1_%d" % dc),), writes=(mb("sbst"),))
                        s_update(1, banks)
                    if h == 0 and s == 0:
                        checkpoint("rA4")
                    ktok(base, 0, kn[0] % 2)
                    for n in range(cps):
                        gn = base + n
                        ps_s = next_ps()
                        mm_group(ps_s, psum[ps_s][:, 0:128],
                                 [(kT[:, dc, gn * CH:(gn + 1) * CH], qT[:, dc, gn * CH:(gn + 1) * CH]) for dc in range(2)],
                                 reads=(mb("kT"), mb("qT")))
                        sl = gn % 2
                        T.op("dve", lambda e, ps_s=ps_s, sl=sl, h=h: e.tensor_tensor(sTm[sl][:, :], psum[ps_s][:, 0:128], maskfb[:, h, :], ALU.mult),
                             reads=(B_ps[ps_s], mb("maskfb")), writes=(B_sT[sl],))
                        ks = kn[0] % 2
                        kn[0] += 1
                        if n + 1 < cps:
                            ktok(gn + 1, 0, kn[0] % 2)
                        banks = ds_mm(gn, 0, ks)
                        Sbf = Sbf2[gn % 2]
                        bsbf = mb("Sbf%d" % (gn % 2))
                        for dc in range(2):
                            T.op("act", lambda e: e.activation(Sbf[:, dc, :], Sst[:, 0, dc, :], AF.Copy),
                                 reads=(mb("S0_%d" % dc),), writes=(bsbf,))
                        s_update(0, banks)
                        po = next_ps()
                        pairs = [(sTm[sl][:, :], vtok[:, gn, :])]
                        for dc in range(2):
                            pairs.append((qb[:, dc, gn * CH:(gn + 1) * CH], sbst2[:, 2 * gn + dc, :]))
                        for dc in range(2):
                            pairs.append((qf[:, dc, gn * CH:(gn + 1) * CH], Sbf[:, dc, :]))
                        mm_group(po, psum[po][:, :], pairs, reads=(B_sT[sl], mb("vtok"), mb("qf"), mb("qb"), bsbf, mb("sbst")))
                        T.op("act", lambda e, po=po, gn=gn, par=par: e.activation(oloc[par][:, gn, :], psum[po][:, :], AF.Copy),
                             reads=(B_ps[po],), writes=(B_ol[par],))
                    if h == 0 and s == 0:
                        checkpoint("rA5")
                    if cfg.state_out:
                        T.dma("sp", lambda e, s=s, h=h: [e.dma_start(out=st_d[s, d, h].rearrange("(c p) e -> p c e", p=128), in_=Sst[:, d, :, :])
                                                         for d in range(2)],
                              sem_st[s % 2], reads=tuple(SB_all), writes=(mb("stout"),), n=2)
                    else:
                        T.dma("sp", lambda e, h=h: [e.dma_start(
                            out=exin[h][d * 256:(d + 1) * 256, :].rearrange("(c p) e -> p c e", p=128),
                            in_=Sst[:, d, :, :]) for d in range(2)],
                            sem_st[0], reads=tuple(SB_all), writes=(mb("exin%d" % h),), n=2)
                T.dma("sp", lambda e, h=h, par=par: [e.dma_start(out=olsp[h, :, 0:NT * DV].rearrange("p (n e) -> p n e", e=DV),
                                                                 in_=oloc[par][:, 0:NT, :])],
                      sem_ol[par], reads=(B_ol[par],), writes=(mb("olsp%d" % h),))
            T.barrier()
        checkpoint("retA")
        if cfg.exchange:
            exchange_head(NH - 1)
        checkpoint("retX")
        if True:
            olb = [xflat[:, i * 4096:(i + 1) * 4096].rearrange("p (n e) -> p n e", e=DV) for i in range(2)]
            posL = xflat[:, 8192:9216]
            Gt2 = [xflat[:, 9216:11264].rearrange("p (d c e) -> p d c e", d=2, c=2),
                   xflat[:, 13400:15448].rearrange("p (d c e) -> p d c e", d=2, c=2)]
            ot = [xflat[:, 11264 + i * 512:11264 + (i + 1) * 512] for i in range(2)]
            sgt = [xflat[:, 12288 + i * 512:12288 + (i + 1) * 512] for i in range(2)]
            bst = xflat[:, 13312:13320]
            tabf = tmpA
            tabb = tmpB
            qTb = [bflat[:, i * 2048:(i + 1) * 2048].rearrange("p (a t) -> p a t", a=2)[:, :, 0:Tn] for i in range(2)]
            qlf = bflat[:, 4096:6144].rearrange("p (a t) -> p a t", a=2)[:, :, 0:Tn]
            qlb = bflat[:, 6144:8192].rearrange("p (a t) -> p a t", a=2)[:, :, 0:Tn]
            Gb = bflat[:, 8192:10240].rearrange("p (d c e) -> p d c e", d=2, c=2)
            gat = [bflat[:, 10240 + i * 512:10240 + (i + 1) * 512] for i in range(2)]
            gT1 = bflat[:, 11264:15360].rearrange("p (f t) -> p f t", f=4)[:, :, 0:Tn]
            gT = [gT1, gT1]
            cview["posL1"] = posL
            sem_tb2 = T.new_dma_sem()
            T.dma("sp", lambda e: [e.dma_start(out=posL, in_=cst_d[:, COFF["posL1"]:COFF["posL1"] + 1024])],
                  sem_tb2, writes=(mb("cst"),))
            B_olb = [Buf("olb0"), Buf("olb1")]
            B_qTb = [Buf("qTb0"), Buf("qTb1")]
            B_gat = [Buf("gat0"), Buf("gat1")]
            _bgt = Buf("gT")
            B_gT = [_bgt, _bgt]
            B_ot = [Buf("ot0"), Buf("ot1")]
            B_sgt = [Buf("sgt0"), Buf("sgt1")]
            sem_l = [T.new_dma_sem() for _ in range(2)]
            sem_g = T.new_dma_sem()
            sem_gt = [T.new_dma_sem() for _ in range(2)]

            sem_g2 = [T.new_dma_sem() for _ in range(2)]

            def load_head(h):
                par = h % 2
                if cfg.exchange:
                    r_f = 0 * 512 + 0 * 256
                    r_b = 1 * 512 + 1 * 256
                    T.dma("sp", lambda e: [
                        e.dma_start(out=Gt2[par][:, 0, :, :], in_=exout[h][r_f:r_f + 256, :].rearrange("(c p) e -> p c e", p=128)),
                        e.dma_start(out=Gt2[par][:, 1, :, :], in_=exout[h][r_b:r_b + 256, :].rearrange("(c p) e -> p c e", p=128))],
                        sem_g2[par], reads=(mb("exout%d" % h),), writes=(mb("Gt%d" % par),), n=2)
                fns = [lambda e, h=h, par=par: e.dma_start(out=olb[par][:, 0:NT, :], in_=olsp[h, :, 0:NT * DV].rearrange("p (n e) -> p n e", e=DV))]
                wr = [B_olb[par]]
                if cfg.exchange:
                    fns.append(lambda e, h=h, par=par: e.dma_start(out=qTb[par], in_=qsp[h, :, 0:2 * Tn].rearrange("p (a t) -> p a t", a=2)))
                    wr.append(B_qTb[par])
                T.dma("sp", lambda e, fns=fns: [f(e) for f in fns], sem_l[par],
                      reads=(mb("olsp%d" % h), mb("qsp%d" % h)), writes=tuple(wr), n=len(fns))

            load_head(0)
            srcsB = []
            for hh in range(NH):
                srcsB += [wtile(w_in_d, 0, 8192 + hh * 512), wtile(w_in_d, 0, 8192 + hh * 512 + 256)]
            wsB = WStream(srcsB, hold=2, lookahead=2)
            for h in range(NH):
                par = h % 2
                if h + 1 < NH:
                    load_head(h + 1)
                sg0 = wsB.get(2 * h)
                sg1 = wsB.get(2 * h + 1)
                if cfg.exchange:
                    Gt = Gt2[par]
                    T.op("dve", lambda e: e.tensor_scalar(Gb[:, 0, :, :], Gt[:, 0, :, :], cs("flags", 1, 1), None, ALU.mult),
                         reads=(mb("Gt%d" % par), mb("cst")), writes=(mb("Gb"),))
                    T.op("dve", lambda e: e.tensor_scalar(Gb[:, 1, :, :], Gt[:, 1, :, :], cs("flags", 1, 0), None, ALU.mult),
                         reads=(mb("Gt%d" % par), mb("cst")), writes=(mb("Gb"),))
                    T.op("act", lambda e, h=h: e.activation(tabf[:, 0:Tn], cs("posL1", Tn), AF.Exp, scale=LG[:, h:h + 1]),
                         reads=(mb("LG"), mb("cst")), writes=(mb("tabf2"),))
                    T.op("act", lambda e, h=h: e.activation(tabb[:, 0:Tn], cs("posL1", Tn), AF.Exp, scale=NLG[:, 8 + h:9 + h],
                                                            bias=LB1025[:, 8 + h:9 + h]),
                         reads=(mb("NLG"), mb("LB1025"), mb("cst")), writes=(mb("tabb2"),))
                    for dc in range(2):
                        T.op("dve", lambda e, dc=dc, par=par: e.tensor_tensor(qlf[:, dc, :], qTb[par][:, dc, :], tabf[:, 0:Tn], ALU.mult),
                             reads=(B_qTb[par], mb("tabf2")), writes=(mb("qlf"),))
                        T.op("dve", lambda e, dc=dc, par=par: e.tensor_tensor(qlb[:, dc, :], qTb[par][:, dc, :], tabb[:, 0:Tn], ALU.mult),
                             reads=(B_qTb[par], mb("tabb2")), writes=(mb("qlb"),))
                mvall = xflat[:, 13320:13320 + 2 * NT].rearrange("p (n t) -> p n t", t=2)
                pcs = {}
                if cfg.exchange:
                    for n in range(NT):
                        pc = next_ps()
                        pairs = []
                        for dc in range(2):
                            pairs.append((qlf[:, dc, n * CH:(n + 1) * CH], Gb[:, 0, dc, :]))
                        for dc in range(2):
                            pairs.append((qlb[:, dc, n * CH:(n + 1) * CH], Gb[:, 1, dc, :]))
                        mm_group(pc, psum[pc][:, :], pairs, reads=(mb("qlf"), mb("qlb"), mb("Gb")))
                        T.op("dve", lambda e: e.tensor_tensor(olb[par][:, n, :], psum[pc][:, :], olb[par][:, n, :], ALU.add),
                             reads=(B_ps[pc], B_olb[par]), writes=(B_olb[par],))
                for n in range(NT):
                    s2 = n % 2
                    stt_ = bst[:, 0:6] if s2 == 0 else small[:, 56:62]
                    T.op("dve", lambda e: e.bn_stats(stt_, olb[par][:, n, :]),
                         reads=(B_olb[par],), writes=(mb("bns%d" % s2),))
                    T.op("dve", lambda e: e.bn_aggr(mvall[:, n, :], stt_), reads=(mb("bns%d" % s2),), writes=(mb("mvall"),))
                T.op("dve", lambda e: e.tensor_scalar(mvall[:, :, 1:2], mvall[:, :, 1:2], LN_EPS, None, ALU.add),
                     reads=(mb("mvall"),), writes=(mb("mvall"),))
                T.op("act", lambda e: e.activation(mvall[:, :, 1:2], mvall[:, :, 1:2], AF.Sqrt),
                     reads=(mb("mvall"),), writes=(mb("mvall"),))
                T.op("dve", lambda e: e.reciprocal(mvall[:, :, 1:2], mvall[:, :, 1:2]),
                     reads=(mb("mvall"),), writes=(mb("mvall"),))

                def p2_s1(n):
                    s2 = n % 2
                    pg = next_ps()
                    for half, si in enumerate((sg0, sg1)):
                        wv = wview(si)
                        pairs = [(hmT[:, k, n * CH:(n + 1) * CH], wv[:, k, :]) for k in range(KC)]
                        mm_group(pg, psum[pg][:, half * 256:(half + 1) * 256], pairs, reads=(B_ring[si],) + tuple(B_hm))
                    T.op("act", lambda e: e.activation(sgt[s2][:, :], psum[pg][:, :], AF.Silu),
                         reads=(B_ps[pg],), writes=(B_sgt[s2],))
                    T.op("dve", lambda e: e.tensor_scalar(ot[s2][:, :], olb[par][:, n, :], mvall[:, n, 0:1], mvall[:, n, 1:2],
                                                          ALU.subtract, ALU.mult),
                         reads=(B_olb[par], mb("mvall")), writes=(B_ot[s2],))
                    T.op("dve", lambda e: e.tensor_tensor(gat[s2][:, :], ot[s2][:, :], sgt[s2][:, :], ALU.mult),
                         reads=(B_ot[s2], B_sgt[s2]), writes=(B_gat[s2],))

                def p2_s2(n):
                    s2 = n % 2
                    pt = next_ps()
                    pt16 = psum[pt][:, 0:256].bitcast(BF16)
                    for fc in range(4):
                        T.op("pe", lambda e: e.transpose(
                            pt16[:, fc * 128:(fc + 1) * 128], gat[s2][:, fc * 128:(fc + 1) * 128], identb[:, :]),
                            reads=(B_gat[s2], mb("identb")), writes=(B_ps[pt],), signal=(fc == 3))
                    T.op("act", lambda e: e.activation(
                        gT[par][:, :, n * CH:(n + 1) * CH], pt16[:, :].rearrange("p (f t) -> p f t", f=4), AF.Copy),
                        reads=(B_ps[pt],), writes=(B_gT[par],))

                p2_s1(0)
                for n in range(NT):
                    if n + 1 < NT:
                        p2_s1(n + 1)
                    p2_s2(n)
                T.dma("sp", lambda e, h=h, par=par: [e.dma_start(out=gtsp[h, :, 0:4 * Tn].rearrange("p (f t) -> p f t", f=4), in_=gT[par])],
                      sem_gt[par], reads=(B_gT[par],), writes=(mb("gtsp"),))
            T.barrier()
        checkpoint("retB")
        sem_c1 = T.new_dma_sem()
        T.dma("sp", lambda e: [e.dma_start(out=bufB[:, 4 * h:4 * h + 4, 0:Tn], in_=gtsp[h, :, 0:4 * Tn].rearrange("p (f t) -> p f t", f=4))
                               for h in range(4)] +
                              [e.dma_start(out=hmT[:, 4 * (h - 4):4 * (h - 4) + 4, 0:Tn], in_=gtsp[h, :, 0:4 * Tn].rearrange("p (f t) -> p f t", f=4))
                               for h in range(4, 8)] +
                              [e.dma_start(out=xT[:, :, 0:Tn], in_=xspill[:, 0:KC * Tn].rearrange("p (k t) -> p k t", k=KC))],
              sem_c1, reads=(mb("gtsp"), mb("xspill")), writes=tuple(B_bb) + tuple(B_hm) + tuple(B_x), n=9)
        prescale_x(cfg, None, None, 0)
        srcs = []
        for mbk in range(8):
            for kh in range(2):
                srcs.append(wtile(w_o_d, kh * 2048, mbk * 256))

        def in_tile(kk, t0, tn):
            return bufB[:, kk, t0:t0 + tn] if kk < 16 else hmT[:, kk - 16, t0:t0 + tn]
        dense_acc_into_x(cfg, srcs, in_tile, tuple(B_bb) + tuple(B_hm), lambda m: modT[:, 1, c, 32 + m:33 + m], nk=32)
        T.barrier()

    def run_pass(cfg):
        c = cfg.cond
        Tn = cfg.T
        sem_x = T.new_dma_sem()
        sem_y = T.new_dma_sem()
        xsrc = cfg.x_d.rearrange("(k p) t -> p k t", p=128)
        if cfg.halo:
            T.dma("sp", lambda e: [e.dma_start(out=xT[:, :, 0:Tn], in_=xsrc[:, :, HALO:HALO + Tn]),
                                   e.dma_start(out=halo[:, :, 0:HALO], in_=xsrc[:, :, 0:HALO]),
                                   e.dma_start(out=halo[:, :, HALO:2 * HALO], in_=xsrc[:, :, HALO + Tn:2 * HALO + Tn])],
                  sem_x, writes=tuple(B_x) + (mb("halo"),), n=3)
        else:
            T.dma("sp", lambda e: [e.dma_start(out=xT[:, :, 0:Tn], in_=xsrc[:, :, 0:Tn])], sem_x, writes=tuple(B_x))
        mod_apply(cfg, lambda k: mder[:, 0, c, k:k + 1], lambda k: modT[:, 0, c, k:k + 1])
        if cfg.halo:
            for k in range(KC):
                T.op("dve", lambda e, k=k: e.tensor_scalar(hmT[:, k, TS:TS + 2 * HALO], halo[:, k, :], mder[:, 0, c, k:k + 1],
                                                           modT[:, 0, c, k:k + 1], ALU.mult, ALU.add),
                     reads=(mb("halo"), mb("mder"), mb("modT")), writes=(mb("hmhalo"),))
        checkpoint("modapply")
        conv_mixer(cfg)
        checkpoint("conv")
        layer_norm(cfg, "ln1_g", "ln1_b", 0, (0, c))
        checkpoint("ln1")
        ffn(cfg, 0)
        checkpoint("ffn0")
        layer_norm(cfg, "ln2_g", "ln2_b", 0, None)
        checkpoint("l0")
        mod_pump(96)
        sem_spill = T.new_dma_sem()
        T.dma("sp", lambda e: [e.dma_start(out=xspill[:, 0:KC * Tn].rearrange("p (k t) -> p k t", k=KC), in_=xT[:, :, 0:Tn])],
              sem_spill, reads=tuple(B_x), writes=(mb("xspill"),))
        mod_apply(cfg, lambda k: mder[:, 1, c, k:k + 1], lambda k: modT[:, 1, c, k:k + 1])
        retention(cfg)
        checkpoint("ret")
        layer_norm(cfg, "ln1_g", "ln1_b", 16, (1, c))
        ffn(cfg, 1)
        layer_norm(cfg, "ln2_g", "ln2_b", 16, None)
        checkpoint("l1")
        T.dma("sp", lambda e: [e.dma_start(out=cfg.y_d.rearrange("(k p) t -> p k t", p=128), in_=xT[:, :, 0:Tn])],
              sem_y, reads=tuple(B_x), writes=(mb("yout"),))
        T.barrier()

    cs_ = Cfg()
    cs_.T = TS
    cs_.segs = [(0, TS)]
    cs_.cond = 0
    cs_.halo = True
    cs_.rope = True
    cs_.init_state = True
    cs_.state_out = False
    cs_.exchange = True
    cs_.x_d = xs_d
    cs_.y_d = ys_d
    cp_ = Cfg()
    cp_.T = TP
    cp_.segs = [(0, 256), (256, 256)]
    cp_.cond = 1
    cp_.halo = False
    cp_.rope = False
    cp_.init_state = False
    cp_.state_out = True
    cp_.exchange = False
    cp_.x_d = xp_d
    cp_.y_d = yp_d

    try:
        if _stop_now:
            raise StopBuild()
        run_pass(cs_)
        checkpoint("pass_s")
        run_pass(cp_)
    except StopBuild:
        pass
    T.barrier()
    if stop is not None:
        sem_dbg = T.new_dma_sem()
        src = dbg(locals())
        T.dma("sp", lambda e: [e.dma_start(out=dbg_d[:, 0:src.shape[1]], in_=src)], sem_dbg)
        T.barrier()

    with nc.Block() as block:
        @block.tensor
        def _(e):
            T.replay("pe", e)

        @block.scalar
        def _(e):
            T.replay("act", e)

        @block.vector
        def _(e):
            T.replay("dve", e)

        @block.gpsimd
        def _(e):
            T.replay("pool", e)

        @block.sync
        def _(e):
            T.replay("sp", e)
    es.close()
    return nc


def _chunkT(v):
    v = np.asarray(v, np.float32).reshape(-1, 128)
    return np.ascontiguousarray(v.T)


def _consts(rank):
    cst = np.zeros((128, NCST), np.float32)
    p = np.arange(128)
    cst[:, COFF["ident"]:COFF["ident"] + 128] = np.eye(128, dtype=np.float32)
    perm = np.zeros((128, 128), np.float32)
    perm[p, (p + 64) % 128] = 1.0
    cst[:, COFF["perm"]:COFF["perm"] + 128] = perm
    dmat = (p[None, :] - p[:, None]).astype(np.float32)
    cst[:, COFF["Pm"]:COFF["Pm"] + 128] = np.maximum(dmat, 0)
    cst[:, COFF["Nm"]:COFF["Nm"] + 128] = np.maximum(-dmat, 0)
    cst[:, COFF["indF"]:COFF["indF"] + 128] = (dmat >= 0)
    cst[:, COFF["indB"]:COFF["indB"] + 128] = (dmat <= 0)
    t = np.arange(TS)
    cst[:, COFF["posC1"]:COFF["posC1"] + TS] = ((t % CH) + 1)[None, :]
    cst[:, COFF["posL1"]:COFF["posL1"] + TS] = (t + 1)[None, :]
    tg = rank * TS + t
    row = (tg // 64).astype(np.float32)
    col = (tg % 64).astype(np.float32)
    inv = (10000.0 ** (-(np.arange(64, dtype=np.float32)) / 64.0)).astype(np.float32)
    invp = inv[p % 64]
    sgn = np.where(p < 64, -1.0, 1.0).astype(np.float32)
    a0 = (row[None, :] * invp[:, None]).astype(np.float32)
    a1 = (col[None, :] * invp[:, None]).astype(np.float32)
    cst[:, COFF["cos0"]:COFF["cos0"] + TS] = np.cos(a0)
    cst[:, COFF["sin0"]:COFF["sin0"] + TS] = np.sin(a0) * sgn[:, None]
    cst[:, COFF["cos1"]:COFF["cos1"] + TS] = np.cos(a1)
    cst[:, COFF["sin1"]:COFF["sin1"] + TS] = np.sin(a1) * sgn[:, None]
    cst[:, COFF["pcol"]] = 127 - p
    cst[:, COFF["pcol"] + 1] = p
    cst[:, COFF["flags"]] = 1.0 if rank == 0 else 0.0
    cst[:, COFF["flags"] + 1] = 1.0 if rank == 1 else 0.0
    cst[:, COFF["ones"]:COFF["ones"] + 128] = 1.0
    return cst


_NC_CACHE = {}


def kernel(x_prompt, x_sample, state_ret, c, c_ctx, w_mod, b_mod, ln1_g, ln1_b, ln2_g, ln2_b,
           w_pw1, b_pw1, w_dw, b_dw, cn_g, cn_b, w_pw2, b_pw2,
           w_ret_in, ret_log2_rate, w_ret_o, w_ff1, b_ff1, w_ff2, b_ff2):
    if "nc" not in _NC_CACHE:
        _NC_CACHE["nc"] = build_program()
    nc = _NC_CACHE["nc"]
    in_maps = make_in_maps(x_prompt, x_sample, state_ret, c, c_ctx, w_mod, b_mod, ln1_g, ln1_b, ln2_g, ln2_b,
                           w_pw1, b_pw1, w_dw, b_dw, cn_g, cn_b, w_pw2, b_pw2,
                           w_ret_in, ret_log2_rate, w_ret_o, w_ff1, b_ff1, w_ff2, b_ff2)
    res = run_bass_kernel_spmd(nc, in_maps, core_ids=list(range(8)))
    return assemble(res.results)


def make_in_maps(x_prompt, x_sample, state_ret, c, c_ctx, w_mod, b_mod, ln1_g, ln1_b, ln2_g, ln2_b,
                 w_pw1, b_pw1, w_dw, b_dw, cn_g, cn_b, w_pw2, b_pw2,
                 w_ret_in, ret_log2_rate, w_ret_o, w_ff1, b_ff1, w_ff2, b_ff2):
    f = lambda a: np.ascontiguousarray(np.asarray(a, dtype=np.float32))
    x_prompt, x_sample, state_ret = f(x_prompt), f(x_sample), f(state_ret)
    vec = np.concatenate([
        _chunkT(f(b_mod)), _chunkT(f(ln1_g)), _chunkT(f(ln1_b)), _chunkT(f(ln2_g)), _chunkT(f(ln2_b)),
        _chunkT(f(b_pw1)), _chunkT(f(w_dw)), _chunkT(f(b_dw)), _chunkT(f(cn_g)), _chunkT(f(cn_b)),
        _chunkT(f(b_pw2)), _chunkT(f(b_ff1)), _chunkT(f(b_ff2))], axis=1)
    assert vec.shape == (128, NV)
    rate = np.ascontiguousarray(np.broadcast_to(f(ret_log2_rate).reshape(1, 16), (128, 16)))
    shared = dict(w_mod=f(w_mod), w_pw1=f(w_pw1)[0], w_pw2=f(w_pw2)[0], w_ret_in=f(w_ret_in)[0],
                  w_ret_o=f(w_ret_o)[0], w_ff1=f(w_ff1), w_ff2=f(w_ff2), vecT=vec, rate=rate)
    xs_pad = np.zeros((x_sample.shape[0], 2048 + 2 * HALO, D), np.float32)
    xs_pad[:, HALO:HALO + 2048] = x_sample
    in_maps = []
    for core in range(8):
        b, r = core // 2, core % 2
        xs = np.ascontiguousarray(xs_pad[b, r * TS:r * TS + TS + 2 * HALO].T)
        xp = np.ascontiguousarray(x_prompt[2 * core:2 * core + 2].reshape(TP, D).T)
        cond = np.stack([f(c)[b], f(c_ctx)], axis=0)
        condT = np.ascontiguousarray(cond.reshape(2, KC, 128).transpose(2, 1, 0).reshape(128, KC * 2))
        m = dict(shared)
        m.update(xs=xs, xp=xp, s0=np.ascontiguousarray(state_ret[b, 0]), condT=condT, cst=_consts(r))
        in_maps.append(m)
    return in_maps


def assemble(results):
    y_prompt = np.zeros((16, 256, D), np.float32)
    y_sample = np.zeros((4, 2048, D), np.float32)
    new_state = np.zeros((16, 1, 2, NH, DK, DV), np.float32)
    for core in range(8):
        b, r = core // 2, core % 2
        o = results[core]
        y_sample[b, r * TS:(r + 1) * TS] = o["ys"].T
        y_prompt[2 * core:2 * core + 2] = o["yp"].T.reshape(2, 256, D)
        new_state[2 * core:2 * core + 2, 0] = o["st"]
    return (y_prompt, y_sample, new_state)
```

```python
import math
from contextlib import ExitStack

import numpy as np
import concourse.bass as bass
import concourse.mybir as mybir
from concourse.bass_utils import run_bass_kernel_spmd

F32 = mybir.dt.float32
BF16 = mybir.dt.bfloat16
AF = mybir.ActivationFunctionType
ALU = mybir.AluOpType
AX = mybir.AxisListType

D = 2048
KC = 16
DFF = 8192
NH = 8
DK = 256
DV = 512
CH = 128
CONVW = 31
HALO = 15
ALPHA = (2.0 * 2) ** 0.25
LN_EPS = 1e-5
TS = 1024
TP = 512
NS = 4
ENG = ["pe", "act", "dve", "pool", "sp"]

VOFF = {}
_o = 0
for _n, _c in [("b_mod", 2 * 96), ("ln1_g", 32), ("ln1_b", 32), ("ln2_g", 32), ("ln2_b", 32),
               ("b_pw1", 32), ("w_dw", CONVW * 16), ("b_dw", 16), ("cn_g", 16), ("cn_b", 16),
               ("b_pw2", 16), ("b_ff1", 128), ("b_ff2", 32)]:
    VOFF[_n] = _o
    _o += _c
NV = _o

COFF = {}
_o = 0
for _n, _c in [("ident", 128), ("perm", 128), ("Pm", 128), ("Nm", 128), ("indF", 128), ("indB", 128),
               ("posC1", 1024), ("posL1", 1024), ("cos0", 1024), ("sin0", 1024), ("cos1", 1024),
               ("sin1", 1024), ("pcol", 2), ("flags", 4), ("ones", 128)]:
    COFF[_n] = _o
    _o += _c
NCST = _o


class Buf:
    __slots__ = ("name", "w", "r", "excl")

    def __init__(self, name, excl=False):
        self.name = name
        self.w = None
        self.r = {}
        self.excl = excl


class _Rec:
    def __init__(self):
        self.calls = []

    def __getattr__(self, name):
        def f(*a, **kw):
            self.calls.append((name, a, kw))
            return None
        return f


def _record(fn):
    r = _Rec()
    fn(r)
    return r.calls


class Tracker:
    def __init__(self, nc, es, n_dma_sems=64):
        self.nc = nc
        self.items = {e: [] for e in ENG}
        self.cnt = {e: 0 for e in ENG}
        self.pending = {e: False for e in ENG}
        self.sem = {e: es.enter_context(nc.semaphore("c_" + e)) for e in ENG}
        self.waited = {e: {} for e in ENG}
        self.dsem = [es.enter_context(nc.semaphore("d%d" % i)) for i in range(n_dma_sems)]
        self.dval = [0] * n_dma_sems
        self.dnext = 0

    def new_dma_sem(self):
        i = self.dnext
        self.dnext += 1
        assert i < len(self.dsem)
        return i

    def _wait(self, eng, key, val):
        if self.waited[eng].get(key, 0) >= val:
            return
        self.waited[eng][key] = val
        self.items[eng].append(("wait", key, val))

    def _deps(self, eng, reads, writes):
        for b in reads:
            if b.w is not None:
                k, v = b.w
                if k == eng and eng == "pe":
                    continue
                self._wait(eng, k, v)
            if b.excl:
                for k, v in b.r.items():
                    if k != eng:
                        self._wait(eng, k, v)
        for b in writes:
            if b.w is not None:
                k, v = b.w
                if not (k == eng and eng == "pe"):
                    self._wait(eng, k, v)
            for k, v in b.r.items():
                if not (k == eng and eng == "pe"):
                    self._wait(eng, k, v)

    def _mark(self, ev, reads, writes):
        k, v = ev
        for b in reads:
            if b.r.get(k, 0) < v:
                b.r[k] = v
        for b in writes:
            b.w = ev
            b.r = {}

    def op(self, eng, fn, reads=(), writes=(), signal=True):
        self._deps(eng, reads, writes)
        if signal:
            self.cnt[eng] += 1
            ev = (eng, self.cnt[eng])
            self.pending[eng] = False
        else:
            ev = (eng, self.cnt[eng] + 1)
            self.pending[eng] = True
        calls = _record(fn)
        assert len(calls) == 1
        self.items[eng].append(("op", calls[0], signal))
        self._mark(ev, reads, writes)

    def dma(self, eng, fn, semi, reads=(), writes=(), n=1):
        self._deps(eng, reads, writes)
        key = ("d", semi)
        if self.dval[semi] > 0:
            self._wait(eng, key, self.dval[semi])
        calls = _record(fn)
        assert len(calls) == n, (len(calls), n)
        self.dval[semi] += 16 * n
        ev = (key, self.dval[semi])
        self.items[eng].append(("dma", calls, semi))
        self._mark(ev, reads, writes)

    def barrier(self):
        for e in ENG:
            assert not self.pending[e], e
        for e in ENG:
            for k in ENG:
                if k != e and self.cnt[k] > 0:
                    self._wait(e, k, self.cnt[k])
            for i in range(self.dnext):
                if self.dval[i] > 0:
                    self._wait(e, ("d", i), self.dval[i])

    def replay(self, eng, e):
        for it in self.items[eng]:
            if it[0] == "wait":
                key, val = it[1], it[2]
                sem = self.sem[key] if isinstance(key, str) else self.dsem[key[1]]
                e.wait_ge(sem, val)
            elif it[0] == "op":
                name, a, kw = it[1]
                ins = getattr(e, name)(*a, **kw)
                if it[2]:
                    ins.then_inc(self.sem[eng], 1)
            else:
                for (name, a, kw) in it[1]:
                    getattr(e, name)(*a, **kw).then_inc(self.dsem[it[2]], 16)


class Cfg:
    pass


class StopBuild(Exception):
    pass


def build_program(stop=None, dbg=None, n_cores=8):
    nc = bass.Bass("TRN2", target_bir_lowering=False)
    es = ExitStack()
    dbg_d = nc.dram_tensor("dbg", [128, 4096], F32, kind="ExternalOutput").ap() if stop is not None else None

    def din(name, shape, dt=F32):
        return nc.dram_tensor(name, list(shape), dt, kind="ExternalInput").ap()

    def dout(name, shape, dt=F32):
        return nc.dram_tensor(name, list(shape), dt, kind="ExternalOutput").ap()

    xs_d = din("xs", [D, TS + 2 * HALO])
    xp_d = din("xp", [D, TP])
    s0_d = din("s0", [2, NH, DK, DV])
    condT_d = din("condT", [128, KC * 2])
    vecT_d = din("vecT", [128, NV])
    cst_d = din("cst", [128, NCST])
    rate_d = din("rate", [128, 16])
    w_mod_d = din("w_mod", [2, D, 6 * D])
    w_pw1_d = din("w_pw1", [D, 2 * D])
    w_pw2_d = din("w_pw2", [D, D])
    w_in_d = din("w_ret_in", [D, 6 * D])
    w_o_d = din("w_ret_o", [2 * D, D])
    w_ff1_d = din("w_ff1", [2, D, DFF])
    w_ff2_d = din("w_ff2", [2, DFF, D])
    ys_d = dout("ys", [D, TS])
    yp_d = dout("yp", [D, TP])
    st_d = dout("st", [2, 2, NH, DK, DV])

    xspill = nc.dram_tensor("xspill", [128, KC * TS], F32)
    olsp = nc.dram_tensor("olsp", [NH, 128, 8 * DV], F32)
    qsp = nc.dram_tensor("qsp", [NH, 128, 2 * TS], BF16)
    gtsp = nc.dram_tensor("gtsp", [NH, 128, 4 * TS], BF16)
    exin = [nc.dram_tensor("exin%d" % h, [2 * 256, DV], F32) for h in range(NH)]
    exout = [nc.dram_tensor("exout%d" % h, [2 * 2 * 256, DV], F32) for h in range(NH)]

    def sb(name, shape, dt):
        return es.enter_context(nc.sbuf_tensor(name, list(shape), dt))

    xT = sb("xT", [128, KC, TS], F32)
    hmT = sb("hmT", [128, KC, TS + 32], BF16)
    bufB = sb("bufB", [128, KC, TS], BF16)
    ring = sb("ring", [128, NS, 4096], BF16)
    vecT = sb("vecT_s", [128, NV], F32)
    cst = sb("cst_s", [128, 8], F32)
    modT = sb("modT", [128, 2, 2, 96], F32)
    mder = sb("mder", [128, 2, 2, 64], F32)
    identb = sb("identb", [128, 128], BF16)
    permb = sb("permb", [128, 128], BF16)
    onesb = sb("onesb", [128, 128], BF16)
    condb = sb("condb", [128, KC * 2], BF16)
    condf = sb("condf", [128, KC * 2], F32)
    halo = sb("halo", [128, KC, 2 * HALO], F32)
    rstdv = sb("rstdv", [128, TS], F32)
    cvv = sb("cvv", [128, TS], F32)
    tmpA = sb("tmpA", [128, TS], F32)
    tmpB = sb("tmpB", [128, TS], F32)
    tmpC = sb("tmpC", [128, 512], F32)
    tmpD = sb("tmpD", [128, 512], F32)
    tb16a = sb("tb16a", [128, 512], BF16)
    tb16b = sb("tb16b", [128, 512], BF16)
    ratet = sb("ratet", [128, 16], F32)
    LG = sb("LG", [128, 16], F32)
    NLG = sb("NLG", [128, 16], F32)
    DCH = sb("DCH", [128, 16], F32)
    DOUT = sb("DOUT", [128, 16], F32)
    LB1025 = sb("LB1025", [128, 16], F32)
    LB129 = sb("LB129", [128, 16], F32)
    maskfb = sb("maskfb", [128, NH, 128], F32)
    small = sb("small", [128, 64], F32)

    psum = [es.enter_context(nc.psum_tensor("ps%d" % i, [128, 512], F32)) for i in range(8)]

    T = Tracker(nc, es)

    B_x = [Buf("x%d" % k) for k in range(KC)]
    B_hm = [Buf("hm%d" % k) for k in range(KC)]
    B_bb = [Buf("bb%d" % k) for k in range(KC)]
    B_ring = [Buf("ring%d" % i) for i in range(NS)]
    B_ps = [Buf("ps%d" % i, excl=True) for i in range(8)]
    B_misc = {}

    def mb(name):
        if name not in B_misc:
            B_misc[name] = Buf(name)
        return B_misc[name]

    ring_sem = [T.new_dma_sem() for _ in range(NS)]
    st = {"ring_n": 0, "ps_n": 0}

    def next_ps():
        i = st["ps_n"] % 7
        st["ps_n"] += 1
        return i

    CSMALL = {"pcol": 0, "flags": 2}
    cview = {}

    def cs(name, n=None, off=0):
        if name in CSMALL:
            o = CSMALL[name] + off
            return cst[:, o:o + (n if n is not None else 1)]
        base = cview[name]
        return base[:, off:off + (n if n is not None else 1)]

    def vs(name, idx):
        o = VOFF[name] + idx
        return vecT[:, o:o + 1]

    ring_busy = [False] * NS
    cur_stream = [None]
    mp = {"n": 0}

    def pump_reserve():
        return 1 if 0 < mp["n"] < 96 else 0

    def wload(src_ap):
        i = None
        for j in range(NS):
            cand = (st["ring_n"] + j) % NS
            if not ring_busy[cand]:
                i = cand
                break
        assert i is not None, "no free weight ring slot"
        st["ring_n"] = i + 1
        ring_busy[i] = True
        dst = ring[:, i, :].rearrange("p (k n) -> p k n", k=KC)
        T.dma("pool", lambda e, d=dst, s=src_ap: [e.dma_start(out=d, in_=s)], ring_sem[i],
              reads=(), writes=(B_ring[i],))
        return i

    def wview(i):
        return ring[:, i, :].rearrange("p (k n) -> p k n", k=KC)

    class WStream:
        def __init__(self, srcs, hold=2, lookahead=NS - 2):
            if cur_stream[0] is not None:
                cur_stream[0].close()
            cur_stream[0] = self
            self.srcs = srcs
            self.hold = hold
            self.lookahead = lookahead
            self.issued = 0
            self.released = 0
            self.slots = []
            self.pos = -1
            self._fill()

        def _fill(self):
            while (self.issued < len(self.srcs) and self.issued - (self.pos + 1) < self.lookahead
                   and sum(ring_busy) < NS - pump_reserve()):
                self.slots.append(wload(self.srcs[self.issued]))
                self.issued += 1

        def _release_upto(self, j):
            while self.released <= j and self.released < self.issued:
                ring_busy[self.slots[self.released]] = False
                self.released += 1

        def get(self, j):
            self.pos = max(self.pos, j)
            self._release_upto(j - self.hold)
            if self.issued <= j:
                assert not all(ring_busy), "weight ring exhausted"
                while self.issued <= j:
                    self.slots.append(wload(self.srcs[self.issued]))
                    self.issued += 1
            self._fill()
            return self.slots[j]

        def close(self):
            self._release_upto(len(self.srcs))

    def wtile(w2d, k0, c0):
        return w2d[k0:k0 + D, c0:c0 + 256].rearrange("(k p) n -> p k n", p=128)

    def mm_group(ps_i, ps_ap, pairs, reads):
        n = len(pairs)
        for j, (l, r) in enumerate(pairs):
            T.op("pe", lambda e, o=ps_ap, l=l, r=r, a=(j == 0), z=(j == n - 1): e.matmul(o, l, r, start=a, stop=z),
                 reads=reads if j == 0 else (), writes=(B_ps[ps_i],), signal=(j == n - 1))

    sem_c = T.new_dma_sem()
    xflat = xT[:, :, :].rearrange("p k t -> p (k t)")
    bflat = bufB[:, :, :].rearrange("p k t -> p (k t)")
    cstA = xflat[:, 0:896]
    _o = 0
    for _n in ("ident", "perm", "Pm", "Nm", "indF", "indB", "ones"):
        cview[_n] = cstA[:, _o:_o + 128]
        _o += 128
    T.dma("sp", lambda e: [e.dma_start(out=vecT[:, :], in_=vecT_d[:, :]),
                           e.dma_start(out=cst[:, 0:6], in_=cst_d[:, COFF["pcol"]:COFF["pcol"] + 6]),
                           e.dma_start(out=cstA[:, 0:768], in_=cst_d[:, COFF["ident"]:COFF["ident"] + 768]),
                           e.dma_start(out=cstA[:, 768:896], in_=cst_d[:, COFF["ones"]:COFF["ones"] + 128]),
                           e.dma_start(out=condf[:, :], in_=condT_d[:, :]),
                           e.dma_start(out=ratet[:, :], in_=rate_d[:, :])], sem_c,
          writes=(mb("vec"), mb("cst"), mb("condf"), mb("rate")), n=6)
    T.op("dve", lambda e: e.tensor_copy(identb[:, :], cs("ident", 128)), reads=(mb("cst"),), writes=(mb("identb"),))
    T.op("dve", lambda e: e.tensor_copy(permb[:, :], cs("perm", 128)), reads=(mb("cst"),), writes=(mb("permb"),))
    T.op("dve", lambda e: e.tensor_copy(onesb[:, :], cs("ones", 128)), reads=(mb("cst"),), writes=(mb("onesb"),))
    T.op("act", lambda e: e.activation(condb[:, :], condf[:, :], AF.Silu), reads=(mb("condf"),), writes=(mb("condb"),))

    T.op("act", lambda e: e.activation(small[:, 0:16], ratet[:, :], AF.Exp, scale=math.log(2.0)),
         reads=(mb("rate"),), writes=(mb("small"),))
    xx = small[:, 0:16]
    pp = small[:, 16:32]
    T.op("dve", lambda e: e.tensor_scalar(pp, xx, 0.2, 0.25, ALU.mult, ALU.add), reads=(mb("small"),), writes=(mb("small2"),))
    for cc in (1.0 / 3.0, 0.5, 1.0):
        T.op("dve", lambda e: e.tensor_tensor(pp, pp, xx, ALU.mult), reads=(mb("small2"), mb("small")), writes=(mb("small2"),))
        T.op("dve", lambda e, cc=cc: e.tensor_scalar(pp, pp, cc, None, ALU.add), reads=(mb("small2"),), writes=(mb("small2"),))
    T.op("dve", lambda e: e.tensor_tensor(NLG[:, :], pp, xx, ALU.mult), reads=(mb("small2"), mb("small")), writes=(mb("NLG"),))
    T.op("dve", lambda e: e.tensor_scalar(LG[:, :], NLG[:, :], -1.0, None, ALU.mult), reads=(mb("NLG"),), writes=(mb("LG"),))
    T.op("dve", lambda e: e.tensor_scalar(LB1025[:, :], LG[:, :], float(TS + 1), None, ALU.mult), reads=(mb("LG"),), writes=(mb("LB1025"),))
    T.op("dve", lambda e: e.tensor_scalar(LB129[:, :], LG[:, :], float(CH + 1), None, ALU.mult), reads=(mb("LG"),), writes=(mb("LB129"),))
    T.op("act", lambda e: e.activation(DCH[:, :], LG[:, :], AF.Exp, scale=float(CH)), reads=(mb("LG"),), writes=(mb("DCH"),))
    T.op("dve", lambda e: e.tensor_scalar(small[:, 32:40], LG[:, 0:8], cs("pcol", 1, 0), None, ALU.mult),
         reads=(mb("LG"), mb("cst")), writes=(mb("small3"),))
    T.op("dve", lambda e: e.tensor_scalar(small[:, 40:48], LG[:, 8:16], cs("pcol", 1, 1), None, ALU.mult),
         reads=(mb("LG"), mb("cst")), writes=(mb("small3"),))
    T.op("act", lambda e: e.activation(DOUT[:, :], small[:, 32:48], AF.Exp), reads=(mb("small3"),), writes=(mb("DOUT"),))
    for h in range(NH):
        T.op("act", lambda e, h=h: e.activation(tmpC[:, 0:128], cs("Pm", 128), AF.Exp, scale=LG[:, h:h + 1]),
             reads=(mb("LG"), mb("cst")), writes=(mb("tmpC"),))
        T.op("act", lambda e, h=h: e.activation(tmpD[:, 0:128], cs("Nm", 128), AF.Exp, scale=LG[:, 8 + h:9 + h]),
             reads=(mb("LG"), mb("cst")), writes=(mb("tmpD"),))
        T.op("dve", lambda e: e.tensor_tensor(tmpC[:, 0:128], tmpC[:, 0:128], cs("indF", 128), ALU.mult),
             reads=(mb("tmpC"),), writes=(mb("tmpC"),))
        T.op("dve", lambda e: e.tensor_tensor(tmpD[:, 0:128], tmpD[:, 0:128], cs("indB", 128), ALU.mult),
             reads=(mb("tmpD"),), writes=(mb("tmpD"),))
        T.op("dve", lambda e, h=h: e.tensor_tensor(maskfb[:, h, :], tmpC[:, 0:128], tmpD[:, 0:128], ALU.add),
             reads=(mb("tmpC"), mb("tmpD")), writes=(mb("maskfb"),))

    MODB = 7
    mod_tiles = [(l, cb) for l in range(2) for cb in range(48)]

    def mod_evac(l, m0, m1):
        for c in range(2):
            src = psum[MODB][:, l * 192:(l + 1) * 192].rearrange("p (m c) -> p c m", c=2)[:, c, m0:m1]
            T.op("dve", lambda e: e.tensor_tensor(
                modT[:, l, c, m0:m1], src, vecT[:, VOFF["b_mod"] + l * 96 + m0:VOFF["b_mod"] + l * 96 + m1], ALU.add),
                reads=(B_ps[MODB], mb("vec")), writes=(mb("modT"),))
        for c in range(2):
            if m0 <= 16 and m1 >= 32:
                T.op("dve", lambda e: e.tensor_scalar(mder[:, l, c, 0:16], modT[:, l, c, 16:32], 1.0, None, ALU.add),
                     reads=(mb("modT"),), writes=(mb("mder"),))
            if m0 <= 64 and m1 >= 80:
                T.op("dve", lambda e: e.tensor_scalar(mder[:, l, c, 16:32], modT[:, l, c, 64:80], 1.0, None, ALU.add),
                     reads=(mb("modT"),), writes=(mb("mder"),))

    pump_slot = [None]

    def _pump_prefetch():
        if mp["n"] < len(mod_tiles) and pump_slot[0] is None:
            l, cb = mod_tiles[mp["n"]]
            pump_slot[0] = wload(wtile(w_mod_d[l], 0, cb * 256))

    def mod_pump(k):
        for _ in range(k):
            if mp["n"] >= len(mod_tiles):
                return
            _pump_prefetch()
            l, cb = mod_tiles[mp["n"]]
            si = pump_slot[0]
            wv = wview(si)
            for m2 in range(2):
                mchunk = cb * 2 + m2
                pairs = [(wv[:, k_, m2 * 128:(m2 + 1) * 128], condb[:, 2 * k_:2 * k_ + 2]) for k_ in range(KC)]
                col = l * 192 + 2 * mchunk
                mm_group(MODB, psum[MODB][:, col:col + 2], pairs, reads=(B_ring[si], mb("condb")))
            ring_busy[si] = False
            pump_slot[0] = None
            mp["n"] += 1
            _pump_prefetch()
            if cb == 15:
                mod_evac(l, 0, 32)
            if cb == 47:
                mod_evac(l, 32, 96)

    def mod_bulk(k):
        srcs0 = [wtile(w_mod_d[mod_tiles[i][0]], 0, mod_tiles[i][1] * 256) for i in range(k)]
        ws0 = WStream(srcs0, hold=1, lookahead=NS - 1)
        for i in range(k):
            l, cb = mod_tiles[i]
            si = ws0.get(i)
            wv = wview(si)
            for m2 in range(2):
                mchunk = cb * 2 + m2
                pairs = [(wv[:, k_, m2 * 128:(m2 + 1) * 128], condb[:, 2 * k_:2 * k_ + 2]) for k_ in range(KC)]
                col = l * 192 + 2 * mchunk
                mm_group(MODB, psum[MODB][:, col:col + 2], pairs, reads=(B_ring[si], mb("condb")))
        ws0.close()
        cur_stream[0] = None
        mp["n"] = k
        mod_evac(0, 0, 32)

    mod_bulk(16)
    T.barrier()
    _stop_now = (stop == "prologue")

    def checkpoint(name):
        if stop == name:
            raise StopBuild()

    def tiles_of(cfg):
        return [(t0, min(512, cfg.T - t0)) for t0 in range(0, cfg.T, 512)]

    def mod_apply(cfg, scp_fn, sh_fn):
        for k in range(KC):
            if k % 2 == 0:
                T.op("act", lambda e, k=k: e.activation(hmT[:, k, 0:cfg.T], xT[:, k, 0:cfg.T], AF.Identity,
                                                        bias=sh_fn(k), scale=scp_fn(k)),
                     reads=(B_x[k], mb("mder"), mb("modT")), writes=(B_hm[k],))
            else:
                T.op("dve", lambda e, k=k: e.tensor_scalar(hmT[:, k, 0:cfg.T], xT[:, k, 0:cfg.T], scp_fn(k), sh_fn(k),
                                                           ALU.mult, ALU.add),
                     reads=(B_x[k], mb("mder"), mb("modT")), writes=(B_hm[k],))

    def prescale_x(cfg, gvec_fn, bias_name, bias_off):
        for k in range(KC):
            if bias_name is None:
                T.op("dve", lambda e, k=k: e.tensor_scalar(xT[:, k, 0:cfg.T], xT[:, k, 0:cfg.T], ALPHA, None, ALU.mult),
                     reads=(B_x[k],), writes=(B_x[k],))
            else:
                T.op("dve", lambda e, k=k: e.tensor_tensor(small[:, 48 + (k % 8):49 + (k % 8)], gvec_fn(k),
                                                           vs(bias_name, bias_off + k), ALU.mult),
                     reads=(mb("modT"), mb("vec")), writes=(mb("small4_%d" % (k % 8)),))
                T.op("dve", lambda e, k=k: e.tensor_scalar(xT[:, k, 0:cfg.T], xT[:, k, 0:cfg.T], ALPHA,
                                                           small[:, 48 + (k % 8):49 + (k % 8)], ALU.mult, ALU.add),
                     reads=(B_x[k], mb("small4_%d" % (k % 8))), writes=(B_x[k],))

    def dense_acc_into_x(cfg, srcs, in_tile_fn, in_bufs, gvec_fn, nk=KC):
        nkh = nk // KC
        ws = WStream(srcs, hold=nkh, lookahead=NS - 1)
        tl = tiles_of(cfg)
        for mbk in range(8):
            slots = [ws.get(mbk * nkh + kh) for kh in range(nkh)]
            if mp["n"] < 64:
                mod_pump(1)
            for m2 in range(2):
                m = mbk * 2 + m2
                for (t0, tn) in tl:
                    pi = next_ps()
                    pairs = []
                    for kh in range(nkh):
                        wv = wview(slots[kh])
                        for k in range(KC):
                            pairs.append((wv[:, k, m2 * 128:(m2 + 1) * 128], in_tile_fn(kh * KC + k, t0, tn)))
                    mm_group(pi, psum[pi][:, 0:tn], pairs, reads=tuple(B_ring[s] for s in slots) + tuple(in_bufs))
                    T.op("dve", lambda e, pi=pi, m=m, t0=t0, tn=tn: e.scalar_tensor_tensor(
                        xT[:, m, t0:t0 + tn], psum[pi][:, 0:tn], gvec_fn(m), xT[:, m, t0:t0 + tn], ALU.mult, ALU.add),
                        reads=(B_ps[pi], B_x[m], mb("modT")), writes=(B_x[m],))

    def layer_norm(cfg, g_name, b_name, voff, hm_fold):
        tl = tiles_of(cfg)
        for (t0, tn) in tl:
            p1 = next_ps()
            p2 = next_ps()
            for k in range(KC):
                ta = tb16a if k % 2 == 0 else tb16b
                tq = tmpC if k % 2 == 0 else tmpD
                na = "tb16a" if k % 2 == 0 else "tb16b"
                nq = "tmpC" if k % 2 == 0 else "tmpD"
                T.op("dve", lambda e, k=k, ta=ta: e.tensor_copy(ta[:, 0:tn], xT[:, k, t0:t0 + tn]),
                     reads=(B_x[k],), writes=(mb(na),))
                T.op("pe", lambda e, k=k, ta=ta: e.matmul(psum[p1][:, 0:tn], onesb[:, :], ta[:, 0:tn], start=(k == 0), stop=(k == KC - 1)),
                     reads=(mb(na), mb("onesb")), writes=(B_ps[p1],), signal=True)
                tq16 = tq[:, 0:256].bitcast(BF16)
                T.op("act", lambda e, k=k, tq16=tq16: e.activation(tq16[:, 0:tn], xT[:, k, t0:t0 + tn], AF.Square),
                     reads=(B_x[k],), writes=(mb(nq),))
                T.op("pe", lambda e, k=k, tq16=tq16: e.matmul(psum[p2][:, 0:tn], onesb[:, :], tq16[:, 0:tn], start=(k == 0), stop=(k == KC - 1)),
                     reads=(mb(nq), mb("onesb")), writes=(B_ps[p2],), signal=True)
            ln_finish_stats(p1, p2, t0, tn, D)
        ln_apply(cfg, g_name, b_name, voff, hm_fold)

    def ln_finish_stats(p1, p2, t0, tn, dd):
        mean = tmpA[:, t0:t0 + tn]
        msq = tmpB[:, t0:t0 + tn]
        T.op("dve", lambda e: e.tensor_scalar(mean, psum[p1][:, 0:tn], 1.0 / dd, None, ALU.mult),
             reads=(B_ps[p1],), writes=(mb("tmpA"),))
        T.op("dve", lambda e: e.tensor_tensor(msq, mean, mean, ALU.mult), reads=(mb("tmpA"),), writes=(mb("tmpB"),))
        T.op("dve", lambda e: e.scalar_tensor_tensor(msq, psum[p2][:, 0:tn], 1.0 / dd, msq, ALU.mult, ALU.subtract),
             reads=(B_ps[p2], mb("tmpB")), writes=(mb("tmpB"),))
        T.op("dve", lambda e: e.tensor_scalar(msq, msq, LN_EPS, None, ALU.add), reads=(mb("tmpB"),), writes=(mb("tmpB"),))
        T.op("act", lambda e: e.activation(msq, msq, AF.Sqrt), reads=(mb("tmpB"),), writes=(mb("tmpB"),))
        T.op("dve", lambda e: e.reciprocal(rstdv[:, t0:t0 + tn], msq),
             reads=(mb("tmpB"),), writes=(mb("rstdv"),))
        T.op("dve", lambda e: e.scalar_tensor_tensor(cvv[:, t0:t0 + tn], mean, -1.0, rstdv[:, t0:t0 + tn], ALU.mult, ALU.mult),
             reads=(mb("tmpA"), mb("rstdv")), writes=(mb("cvv"),))

    def ln_apply(cfg, g_name, b_name, voff, hm_fold):
        Tn = cfg.T
        if hm_fold is not None:
            l, c = hm_fold
            G2 = mder[:, l, c, 32:48]
            B2 = mder[:, l, c, 48:64]
            gv = vecT[:, VOFF[g_name] + voff:VOFF[g_name] + voff + 16]
            bv = vecT[:, VOFF[b_name] + voff:VOFF[b_name] + voff + 16]
            T.op("dve", lambda e: e.tensor_tensor(G2, gv, mder[:, l, c, 16:32], ALU.mult),
                 reads=(mb("vec"), mb("mder")), writes=(mb("mderG"),))
            T.op("dve", lambda e: e.tensor_tensor(B2, bv, mder[:, l, c, 16:32], ALU.mult),
                 reads=(mb("vec"), mb("mder")), writes=(mb("mderB"),))
            T.op("dve", lambda e: e.tensor_tensor(B2, B2, modT[:, l, c, 48:64], ALU.add),
                 reads=(mb("mderB"), mb("modT")), writes=(mb("mderB"),))
        for k in range(KC):
            T.op("dve", lambda e, k=k: e.tensor_tensor(xT[:, k, 0:Tn], xT[:, k, 0:Tn], rstdv[:, 0:Tn], ALU.mult),
                 reads=(B_x[k], mb("rstdv")), writes=(B_x[k],))
            T.op("dve", lambda e, k=k: e.tensor_tensor(xT[:, k, 0:Tn], xT[:, k, 0:Tn], cvv[:, 0:Tn], ALU.add),
                 reads=(B_x[k], mb("cvv")), writes=(B_x[k],))
            if hm_fold is not None:
                T.op("act", lambda e, k=k: e.activation(hmT[:, k, 0:Tn], xT[:, k, 0:Tn], AF.Identity,
                                                        bias=B2[:, k:k + 1], scale=G2[:, k:k + 1]),
                     reads=(B_x[k], mb("mderG"), mb("mderB")), writes=(B_hm[k],))
            T.op("act", lambda e, k=k: e.activation(xT[:, k, 0:Tn], xT[:, k, 0:Tn], AF.Identity,
                                                    bias=vs(b_name, voff + k), scale=vs(g_name, voff + k)),
                 reads=(B_x[k], mb("vec")), writes=(B_x[k],))

    def ffn(cfg, l):
        c = cfg.cond
        prescale_x(cfg, lambda k: modT[:, l, c, 80 + k:81 + k], "b_ff2", l * 16)
        tl = tiles_of(cfg)
        for q in range(4):
            srcs = [wtile(w_ff1_d[l], 0, q * 2048 + j * 256) for j in range(8)]
            ws = WStream(srcs, hold=1, lookahead=NS - 1)
            for j in range(8):
                si = ws.get(j)
                wv = wview(si)
                if mp["n"] < 64:
                    mod_pump(1)
                for h2 in range(2):
                    hc = j * 2 + h2
                    for (t0, tn) in tl:
                        pi = next_ps()
                        pairs = [(wv[:, k, h2 * 128:(h2 + 1) * 128], hmT[:, k, t0:t0 + tn]) for k in range(KC)]
                        mm_group(pi, psum[pi][:, 0:tn], pairs, reads=(B_ring[si],) + tuple(B_hm))
                        tq = tmpC if (hc + t0 // 512) % 2 == 0 else tmpD
                        nm = "tmpC" if (hc + t0 // 512) % 2 == 0 else "tmpD"
                        T.op("act", lambda e, pi=pi, tq=tq, hc=hc, tn=tn: e.activation(
                            tq[:, 0:tn], psum[pi][:, 0:tn], AF.Relu, bias=vs("b_ff1", l * 64 + q * 16 + hc)),
                            reads=(B_ps[pi], mb("vec")), writes=(mb(nm),))
                        T.op("dve", lambda e, tq=tq, hc=hc, t0=t0, tn=tn: e.tensor_tensor(
                            bufB[:, hc, t0:t0 + tn], tq[:, 0:tn], tq[:, 0:tn], ALU.mult),
                            reads=(mb(nm),), writes=(B_bb[hc],))
            srcs2 = [wtile(w_ff2_d[l], q * 2048, mbk * 256) for mbk in range(8)]
            dense_acc_into_x(cfg, srcs2, lambda k, t0, tn: bufB[:, k, t0:t0 + tn], B_bb,
                             lambda m: modT[:, l, c, 80 + m:81 + m])

    def conv_mixer(cfg):
        c = cfg.cond
        Tn = cfg.T
        nseg = len(cfg.segs)
        L = cfg.segs[0][1]
        LP = L + 2 * HALO
        tl = tiles_of(cfg)
        with ExitStack() as ss:
            upad = [ss.enter_context(nc.sbuf_tensor("upad%d_%d" % (i, Tn), [128, nseg * LP], BF16)) for i in range(2)]
            NDG = 8
            dg = ss.enter_context(nc.sbuf_tensor("dg_%d" % Tn, [128, NDG, 128], BF16))
            B_dg = [Buf("dg%d" % i) for i in range(NDG)]
            sg = [tmpC, tmpD]
            B_up = [Buf("upad0"), Buf("upad1")]
            B_sg = [mb("tmpC"), mb("tmpD")]
            if not cfg.halo:
                for i in range(2):
                    T.op("dve", lambda e, i=i: e.memset(upad[i][:, :], 0.0), writes=(B_up[i],))
            srcs = []
            for i in range(8):
                srcs.append(wtile(w_pw1_d, 0, 256 * i))
                srcs.append(wtile(w_pw1_d, 0, 2048 + 256 * i))
            ws = WStream(srcs)
            stt = {"sgn": 0, "dgn": 0, "slots": None}

            def pw1_chunk(ch):
                i, c2 = ch // 2, ch % 2
                if c2 == 0:
                    stt["slots"] = (ws.get(2 * i), ws.get(2 * i + 1))
                sa, sgi = stt["slots"]
                up = upad[ch % 2]
                bup = B_up[ch % 2]
                up3 = up[:, :].rearrange("p (s l) -> p s l", s=nseg)
                groups = [(t0, tn, None) for (t0, tn) in tl]
                if cfg.halo:
                    groups.append((TS, 2 * HALO, "halo"))
                for (t0, tn, kind) in groups:
                    pa = next_ps()
                    pg = next_ps()
                    wa = wview(sa)
                    wg = wview(sgi)
                    pairs_a = [(wa[:, k, c2 * 128:(c2 + 1) * 128], hmT[:, k, t0:t0 + tn]) for k in range(KC)]
                    pairs_g = [(wg[:, k, c2 * 128:(c2 + 1) * 128], hmT[:, k, t0:t0 + tn]) for k in range(KC)]
                    mm_group(pa, psum[pa][:, 0:tn], pairs_a, reads=(B_ring[sa],) + tuple(B_hm) + (mb("hmhalo"),))
                    mm_group(pg, psum[pg][:, 0:tn], pairs_g, reads=(B_ring[sgi],) + tuple(B_hm) + (mb("hmhalo"),))
                    s_i = stt["sgn"] % 2
                    stt["sgn"] += 1
                    T.op("act", lambda e: e.activation(
                        sg[s_i][:, 0:tn], psum[pg][:, 0:tn], AF.Sigmoid, bias=vs("b_pw1", 16 + ch)),
                        reads=(B_ps[pg], mb("vec")), writes=(B_sg[s_i],))
                    if kind == "halo":
                        for side in range(2):
                            dst = up[:, 0:HALO] if side == 0 else up[:, HALO + L:2 * HALO + L]
                            T.op("dve", lambda e: e.scalar_tensor_tensor(
                                tmpA[:, side * HALO:(side + 1) * HALO], psum[pa][:, side * HALO:(side + 1) * HALO], vs("b_pw1", ch),
                                sg[s_i][:, side * HALO:(side + 1) * HALO], ALU.add, ALU.mult),
                                reads=(B_ps[pa], B_sg[s_i], mb("vec")), writes=(mb("tmpA"),))
                            T.op("dve", lambda e: e.tensor_scalar(
                                dst, tmpA[:, side * HALO:(side + 1) * HALO], cs("flags", 1, 1 if side == 0 else 0), None, ALU.mult),
                                reads=(mb("tmpA"), mb("cst")), writes=(bup,))
                    else:
                        if nseg == 1:
                            dst = up[:, HALO + t0:HALO + t0 + tn]
                            src_a = psum[pa][:, 0:tn]
                            src_s = sg[s_i][:, 0:tn]
                        else:
                            dst = up3[:, :, HALO:HALO + L]
                            src_a = psum[pa][:, 0:tn].rearrange("p (s l) -> p s l", s=nseg)
                            src_s = sg[s_i][:, 0:tn].rearrange("p (s l) -> p s l", s=nseg)
                        T.op("dve", lambda e: e.scalar_tensor_tensor(
                            dst, src_a, vs("b_pw1", ch), src_s, ALU.add, ALU.mult),
                            reads=(B_ps[pa], B_sg[s_i], mb("vec")), writes=(bup,))
                mod_pump(2)

            def conv_chunk(ch):
                up = upad[ch % 2]
                bup = B_up[ch % 2]
                up3 = up[:, :].rearrange("p (s l) -> p s l", s=nseg)
                outs = []
                if nseg == 1:
                    for (t0, tn) in tl:
                        outs.append((next_ps(), tn, (lambda k, t0=t0, tn=tn: up[:, t0 + k:t0 + k + tn]), bufB[:, ch, t0:t0 + tn]))
                else:
                    for sI in range(nseg):
                        outs.append((next_ps(), L, (lambda k, sI=sI: up3[:, sI, k:k + L]), bufB[:, ch, sI * L:(sI + 1) * L]))
                for k in range(CONVW):
                    di = stt["dgn"] % NDG
                    stt["dgn"] += 1
                    T.op("dve", lambda e: e.tensor_scalar(dg[:, di, :], identb[:, :], vs("w_dw", k * 16 + ch), None, ALU.mult),
                         reads=(mb("identb"), mb("vec")), writes=(B_dg[di],))
                    for oi, (pb, n_, rfn, dst) in enumerate(outs):
                        T.op("pe", lambda e: e.matmul(psum[pb][:, 0:n_], dg[:, di, :], rfn(k), start=(k == 0), stop=(k == CONVW - 1)),
                             reads=(B_dg[di], bup), writes=(B_ps[pb],), signal=(k == CONVW - 1 or oi == len(outs) - 1))
                for oi, (pb, n_, rfn, dst) in enumerate(outs):
                    if oi % 2 == 0:
                        T.op("act", lambda e: e.activation(dst, psum[pb][:, 0:n_], AF.Identity, bias=vs("b_dw", ch)),
                             reads=(B_ps[pb], mb("vec")), writes=(B_bb[ch],))
                    else:
                        T.op("dve", lambda e: e.tensor_scalar(dst, psum[pb][:, 0:n_], vs("b_dw", ch), None, ALU.add),
                             reads=(B_ps[pb], mb("vec")), writes=(B_bb[ch],))

            pw1_chunk(0)
            for ch in range(16):
                if ch + 1 < 16:
                    pw1_chunk(ch + 1)
                conv_chunk(ch)
            for (t0, tn) in tl:
                p1 = next_ps()
                p2 = next_ps()
                for k in range(KC):
                    tq = tmpC if k % 2 == 0 else tmpD
                    nq = "tmpC" if k % 2 == 0 else "tmpD"
                    T.op("pe", lambda e, k=k: e.matmul(psum[p1][:, 0:tn], onesb[:, :], bufB[:, k, t0:t0 + tn], start=(k == 0), stop=(k == KC - 1)),
                         reads=(B_bb[k], mb("onesb")), writes=(B_ps[p1],), signal=True)
                    tq16 = tq[:, 0:256].bitcast(BF16)
                    T.op("act", lambda e, k=k, tq16=tq16: e.activation(tq16[:, 0:tn], bufB[:, k, t0:t0 + tn], AF.Square),
                         reads=(B_bb[k],), writes=(mb(nq),))
                    T.op("pe", lambda e, k=k, tq16=tq16: e.matmul(psum[p2][:, 0:tn], onesb[:, :], tq16[:, 0:tn], start=(k == 0), stop=(k == KC - 1)),
                         reads=(mb(nq), mb("onesb")), writes=(B_ps[p2],), signal=True)
                ln_finish_stats(p1, p2, t0, tn, D)
            for k in range(KC):
                ta = tmpA if k % 2 == 0 else tmpB
                bta = mb("tmpA" if k % 2 == 0 else "tmpB")
                T.op("dve", lambda e, k=k, ta=ta: e.tensor_tensor(ta[:, 0:Tn], bufB[:, k, 0:Tn], rstdv[:, 0:Tn], ALU.mult),
                     reads=(B_bb[k], mb("rstdv")), writes=(bta,))
                T.op("dve", lambda e, k=k, ta=ta: e.tensor_tensor(ta[:, 0:Tn], ta[:, 0:Tn], cvv[:, 0:Tn], ALU.add),
                     reads=(bta, mb("cvv")), writes=(bta,))
                T.op("act", lambda e, k=k, ta=ta: e.activation(hmT[:, k, 0:Tn], ta[:, 0:Tn], AF.Silu,
                                                               bias=vs("cn_b", k), scale=vs("cn_g", k)),
                     reads=(bta, mb("vec")), writes=(B_hm[k],))
            prescale_x(cfg, lambda k: modT[:, 0, c, 32 + k:33 + k], "b_pw2", 0)
            srcs2 = [wtile(w_pw2_d, 0, mbk * 256) for mbk in range(8)]
            dense_acc_into_x(cfg, srcs2, lambda k, t0, tn: hmT[:, k, t0:t0 + tn], B_hm,
                             lambda m: modT[:, 0, c, 32 + m:33 + m])

    def retention(cfg):
        c = cfg.cond
        Tn = cfg.T
        NT = Tn // CH
        tl = tiles_of(cfg)
        nseg = len(cfg.segs)
        cps = cfg.segs[0][1] // CH
        sem_sp = T.new_dma_sem()
        sem_sp2 = T.new_dma_sem()
        sem_q = [T.new_dma_sem() for _ in range(2)]
        sem_ol = [T.new_dma_sem() for _ in range(2)]
        sem_s0 = T.new_dma_sem()
        sem_st = [T.new_dma_sem() for _ in range(2)]
        T.barrier()
        def exchange_head(hh):
            T.op("pool", lambda e: e.collective_compute(
                "AllGather", ALU.bypass, replica_groups=[[2 * i, 2 * i + 1] for i in range(n_cores // 2)],
                ins=[exin[hh].ap().opt()], outs=[exout[hh].ap().opt()]), reads=(mb("exin%d" % hh),), writes=(mb("exout%d" % hh),))

        if True:
            qT = bufB[:, 0:2, :]
            kT = bufB[:, 2:4, :]
            qf = bufB[:, 4:6, :]
            qb = bufB[:, 6:8, :]
            vtok = bufB[:, 8:12, :].rearrange("p a (b e) -> p (a b) e", e=DV)
            kfb = [bflat[:, 12288 + i * 512:12288 + (i + 1) * 512].rearrange("p (d k) -> p d k", d=2) for i in range(2)]
            sTm = [bflat[:, 13312 + i * 128:13312 + (i + 1) * 128] for i in range(2)]
            qraw = bflat[:, 13568:14080]
            oloc1 = xflat[:, 0:4096].rearrange("p (n e) -> p n e", e=DV)
            oloc = [oloc1, oloc1]
            ropeT = xflat[:, 4096:8192]
            posC = xflat[:, 8192:9216]
            Sst = xflat[:, 9216:11264].rearrange("p (d c e) -> p d c e", d=2, c=2)
            rtmp = [xflat[:, 11264 + i * 512:11264 + (i + 1) * 512] for i in range(2)]
            sbst2 = xflat[:, 12288:16384].bitcast(BF16).rearrange("p (n e) -> p n e", e=DV)
            Sbf2 = [bflat[:, 14080 + i * 1024:14080 + (i + 1) * 1024].rearrange("p (c e) -> p c e", c=2) for i in range(2)]
            tabf = tmpA
            tabb = tmpB
            cview["posC1"] = posC
            for i_, n_ in enumerate(("cos0", "sin0", "cos1", "sin1")):
                cview[n_] = ropeT[:, i_ * 1024:(i_ + 1) * 1024]
            sem_tb = T.new_dma_sem()
            T.dma("sp", lambda e: [e.dma_start(out=ropeT, in_=cst_d[:, COFF["cos0"]:COFF["cos0"] + 4096]),
                                   e.dma_start(out=posC, in_=cst_d[:, COFF["posC1"]:COFF["posC1"] + 1024])],
                  sem_tb, writes=(mb("cst"),), n=2)
            checkpoint("rA0")
            SB_all = [mb("S0_0"), mb("S0_1"), mb("S1_0"), mb("S1_1")]
            _bol = Buf("ol")
            B_ol = [_bol, _bol]
            B_kfb = [Buf("kfb0"), Buf("kfb1")]
            B_sT = [Buf("sT0"), Buf("sT1")]
            B_rt = [Buf("rt0"), Buf("rt1")]
            rt_n = [0]

            for h in range(NH):
                par = h % 2
                if h == 0:
                    srcsA = []
                    for hh in range(NH):
                        srcsA += [wtile(w_in_d, 0, hh * 256), wtile(w_in_d, 0, 2048 + hh * 256),
                                  wtile(w_in_d, 0, 4096 + hh * 512), wtile(w_in_d, 0, 4096 + hh * 512 + 256)]
                    wsA = WStream(srcsA, hold=2, lookahead=2)
                for which in range(2):
                    si = wsA.get(4 * h + which)
                    wv = wview(si)
                    dstT = qT if which == 0 else kT
                    bd = mb("qT" if which == 0 else "kT")
                    for dc in range(2):
                        for (t0, tn) in tl:
                            pi = next_ps()
                            pairs = [(wv[:, k, dc * 128:(dc + 1) * 128], hmT[:, k, t0:t0 + tn]) for k in range(KC)]
                            mm_group(pi, psum[pi][:, 0:tn], pairs, reads=(B_ring[si],) + tuple(B_hm))
                            scale = (DK ** -0.5) if which == 0 else 1.0
                            if not cfg.rope:
                                T.op("act", lambda e, pi=pi, dstT=dstT, dc=dc, t0=t0, tn=tn, scale=scale: e.activation(
                                    dstT[:, dc, t0:t0 + tn], psum[pi][:, 0:tn], AF.Identity, scale=scale),
                                    reads=(B_ps[pi],), writes=(bd,))
                            else:
                                T.op("act", lambda e, pi=pi, tn=tn, scale=scale: e.activation(
                                    qraw[:, 0:tn], psum[pi][:, 0:tn], AF.Identity, scale=scale),
                                    reads=(B_ps[pi],), writes=(mb("qraw"),))
                                p2 = next_ps()
                                mm_group(p2, psum[p2][:, 0:tn], [(permb[:, :], qraw[:, 0:tn])], reads=(mb("qraw"), mb("permb")))
                                r0 = rt_n[0] % 2
                                rt_n[0] += 1
                                cosn = "cos%d" % dc
                                sinn = "sin%d" % dc
                                T.op("dve", lambda e, pi=pi, r0=r0, t0=t0, tn=tn, cosn=cosn, scale=scale: e.scalar_tensor_tensor(
                                    rtmp[r0][:, 0:tn], psum[pi][:, 0:tn], scale, cs(cosn, tn, t0), ALU.mult, ALU.mult),
                                    reads=(B_ps[pi], mb("cst")), writes=(B_rt[r0],))
                                T.op("dve", lambda e, p2=p2, t0=t0, tn=tn, sinn=sinn: e.tensor_tensor(
                                    tmpC[:, 0:tn], psum[p2][:, 0:tn], cs(sinn, tn, t0), ALU.mult),
                                    reads=(B_ps[p2], mb("cst")), writes=(mb("tmpC"),))
                                T.op("dve", lambda e, r0=r0, dstT=dstT, dc=dc, t0=t0, tn=tn: e.tensor_tensor(
                                    dstT[:, dc, t0:t0 + tn], rtmp[r0][:, 0:tn], tmpC[:, 0:tn], ALU.add),
                                    reads=(B_rt[r0], mb("tmpC")), writes=(bd,))
                if h == 0:
                    checkpoint("rA1")
                T.dma("sp", lambda e, h=h: [e.dma_start(out=qsp[h, :, 0:2 * Tn].rearrange("p (a t) -> p a t", a=2),
                                                        in_=qT[:, :, 0:Tn])],
                      sem_q[par], reads=(mb("qT"),), writes=(mb("qsp%d" % h),))
                if h == 0:
                    checkpoint("rA1b")
                sv0 = wsA.get(4 * h + 2)
                sv1 = wsA.get(4 * h + 3)
                for n in range(NT):
                    pi = next_ps()
                    for half, si in enumerate((sv0, sv1)):
                        wv = wview(si)
                        pairs = [(hmT[:, k, n * CH:(n + 1) * CH], wv[:, k, :]) for k in range(KC)]
                        mm_group(pi, psum[pi][:, half * 256:(half + 1) * 256], pairs, reads=(B_ring[si],) + tuple(B_hm))
                    if n % 2 == 0:
                        T.op("act", lambda e, pi=pi, n=n: e.activation(vtok[:, n, :], psum[pi][:, :], AF.Copy),
                             reads=(B_ps[pi],), writes=(mb("vtok"),))
                    else:
                        T.op("dve", lambda e, pi=pi, n=n: e.tensor_copy(vtok[:, n, :], psum[pi][:, :]),
                             reads=(B_ps[pi],), writes=(mb("vtok"),))
                if h == 0:
                    checkpoint("rA2")
                mod_pump(4)
                if cfg.exchange and h > 0:
                    exchange_head(h - 1)
                T.op("act", lambda e, h=h: e.activation(tabf[:, 0:Tn], cs("posC1", Tn), AF.Exp, scale=LG[:, h:h + 1]),
                     reads=(mb("LG"), mb("cst")), writes=(mb("tabf"),))
                T.op("act", lambda e, h=h: e.activation(tabb[:, 0:Tn], cs("posC1", Tn), AF.Exp, scale=NLG[:, 8 + h:9 + h],
                                                        bias=LB129[:, 8 + h:9 + h]),
                     reads=(mb("NLG"), mb("LB129"), mb("cst")), writes=(mb("tabb"),))
                for dc in range(2):
                    T.op("dve", lambda e, dc=dc: e.tensor_tensor(qf[:, dc, 0:Tn], qT[:, dc, 0:Tn], tabf[:, 0:Tn], ALU.mult),
                         reads=(mb("qT"), mb("tabf")), writes=(mb("qf"),))
                    T.op("dve", lambda e, dc=dc: e.tensor_tensor(qb[:, dc, 0:Tn], qT[:, dc, 0:Tn], tabb[:, 0:Tn], ALU.mult),
                         reads=(mb("qT"), mb("tabb")), writes=(mb("qb"),))

                if h == 0:
                    checkpoint("rA3")

                def ktok(n, d, kslot):
                    pk = next_ps()
                    pk16 = psum[pk][:, 0:128].bitcast(BF16)
                    for dc in range(2):
                        T.op("pe", lambda e, dc=dc, pk16=pk16, n=n: e.transpose(
                            pk16[:, dc * 128:(dc + 1) * 128], kT[:, dc, n * CH:(n + 1) * CH], identb[:, :]),
                            reads=(mb("kT"), mb("identb")), writes=(B_ps[pk],), signal=(dc == 1))
                    T.op("dve", lambda e, pk16=pk16, d=d, kslot=kslot, h=h: e.tensor_scalar(
                        kfb[kslot][:, d, :], pk16[:, :], DOUT[:, d * 8 + h:d * 8 + h + 1], None, ALU.mult),
                        reads=(B_ps[pk], mb("DOUT")), writes=(B_kfb[kslot],))

                def ds_mm(n, d, kslot):
                    banks = []
                    for dc in range(2):
                        pi = next_ps()
                        mm_group(pi, psum[pi][:, :], [(kfb[kslot][:, d, dc * 128:(dc + 1) * 128], vtok[:, n, :])],
                                 reads=(B_kfb[kslot], mb("vtok")))
                        banks.append(pi)
                    return banks

                def s_update(d, banks):
                    for dc, pi in enumerate(banks):
                        T.op("dve", lambda e: e.scalar_tensor_tensor(
                            Sst[:, d, dc, :], Sst[:, d, dc, :], DCH[:, d * 8 + h:d * 8 + h + 1], psum[pi][:, :], ALU.mult, ALU.add),
                            reads=(B_ps[pi], mb("S%d_%d" % (d, dc)), mb("DCH")), writes=(mb("S%d_%d" % (d, dc)),))

                kn = [0]
                for s in range(nseg):
                    base = s * cps
                    if cfg.init_state:
                        T.dma("sp", lambda e, h=h: [e.dma_start(out=Sst[:, d, :, :], in_=s0_d[d, h].rearrange("(c p) e -> p c e", p=128))
                                                    for d in range(2)], sem_s0, writes=tuple(SB_all), n=2)
                        for d in range(2):
                            T.op("dve", lambda e, d=d: e.tensor_scalar(Sst[:, d, :, :], Sst[:, d, :, :], cs("flags", 1, d), None, ALU.mult),
                                 reads=(mb("S%d_0" % d), mb("S%d_1" % d), mb("cst")), writes=(mb("S%d_0" % d), mb("S%d_1" % d)))
                    else:
                        for d in range(2):
                            T.op("dve", lambda e, d=d: e.memset(Sst[:, d, :, :], 0.0), writes=(mb("S%d_0" % d), mb("S%d_1" % d)))
                    order1 = [base + n for n in range(cps - 1, -1, -1)]
                    ktok(order1[0], 1, kn[0] % 2)
                    for i1, gn in enumerate(order1):
                        ks = kn[0] % 2
                        kn[0] += 1
                        if i1 + 1 < len(order1):
                            ktok(order1[i1 + 1], 1, kn[0] % 2)
                        banks = ds_mm(gn, 1, ks)
                        for dc in range(2):
                            T.op("act", lambda e: e.activation(sbst2[:, 2 * gn + dc, :], Sst[:, 1, dc, :], AF.Copy),
                                 reads=(mb("S1_%d" % dc),), writes=(mb("sbst"),))
                        s_update(1, banks)
                    if h == 0 and s == 0:
                        checkpoint("rA4")
                    ktok(base, 0, kn[0] % 2)
                    for n in range(cps):
                        gn = base + n
                        ps_s = next_ps()
                        mm_group(ps_s, psum[ps_s][:, 0:128],
                                 [(kT[:, dc, gn * CH:(gn + 1) * CH], qT[:, dc, gn * CH:(gn + 1) * CH]) for dc in range(2)],
                                 reads=(mb("kT"), mb("qT")))
                        sl = gn % 2
                        T.op("dve", lambda e, ps_s=ps_s, sl=sl, h=h: e.tensor_tensor(sTm[sl][:, :], psum[ps_s][:, 0:128], maskfb[:, h, :], ALU.mult),
                             reads=(B_ps[ps_s], mb("maskfb")), writes=(B_sT[sl],))
                        ks = kn[0] % 2
                        kn[0] += 1
                        if n + 1 < cps:
                            ktok(gn + 1, 0, kn[0] % 2)
                        banks = ds_mm(gn, 0, ks)
                        Sbf = Sbf2[gn % 2]
                        bsbf = mb("Sbf%d" % (gn % 2))
                        for dc in range(2):
                            T.op("act", lambda e: e.activation(Sbf[:, dc, :], Sst[:, 0, dc, :], AF.Copy),
                                 reads=(mb("S0_%d" % dc),), writes=(bsbf,))
                        s_update(0, banks)
                        po = next_ps()
                        pairs = [(sTm[sl][:, :], vtok[:, gn, :])]
                        for dc in range(2):
                            pairs.append((qb[:, dc, gn * CH:(gn + 1) * CH], sbst2[:, 2 * gn + dc, :]))
                        for dc in range(2):
                            pairs.append((qf[:, dc, gn * CH:(gn + 1) * CH], Sbf[:, dc, :]))
                        mm_group(po, psum[po][:, :], pairs, reads=(B_sT[sl], mb("vtok"), mb("qf"), mb("qb"), bsbf, mb("sbst")))
                        T.op("act", lambda e, po=po, gn=gn, par=par: e.activation(oloc[par][:, gn, :], psum[po][:, :], AF.Copy),
                             reads=(B_ps[po],), writes=(B_ol[par],))
                    if h == 0 and s == 0:
                        checkpoint("rA5")
                    if cfg.state_out:
                        T.dma("sp", lambda e, s=s, h=h: [e.dma_start(out=st_d[s, d, h].rearrange("(c p) e -> p c e", p=128), in_=Sst[:, d, :, :])
                                                         for d in range(2)],
                              sem_st[s % 2], reads=tuple(SB_all), writes=(mb("stout"),), n=2)
                    else:
                        T.dma("sp", lambda e, h=h: [e.dma_start(
                            out=exin[h][d * 256:(d + 1) * 256, :].rearrange("(c p) e -> p c e", p=128),
                            in_=Sst[:, d, :, :]) for d in range(2)],
                            sem_st[0], reads=tuple(SB_all), writes=(mb("exin%d" % h),), n=2)
                T.dma("sp", lambda e, h=h, par=par: [e.dma_start(out=olsp[h, :, 0:NT * DV].rearrange("p (n e) -> p n e", e=DV),
                                                                 in_=oloc[par][:, 0:NT, :])],
                      sem_ol[par], reads=(B_ol[par],), writes=(mb("olsp%d" % h),))
            T.barrier()
        checkpoint("retA")
        if cfg.exchange:
            exchange_head(NH - 1)
        checkpoint("retX")
        if True:
            olb = [xflat[:, i * 4096:(i + 1) * 4096].rearrange("p (n e) -> p n e", e=DV) for i in range(2)]
            posL = xflat[:, 8192:9216]
            Gt2 = [xflat[:, 9216:11264].rearrange("p (d c e) -> p d c e", d=2, c=2),
                   xflat[:, 13400:15448].rearrange("p (d c e) -> p d c e", d=2, c=2)]
            ot = [xflat[:, 11264 + i * 512:11264 + (i + 1) * 512] for i in range(2)]
            sgt = [xflat[:, 12288 + i * 512:12288 + (i + 1) * 512] for i in range(2)]
            bst = xflat[:, 13312:13320]
            tabf = tmpA
            tabb = tmpB
            qTb = [bflat[:, i * 2048:(i + 1) * 2048].rearrange("p (a t) -> p a t", a=2)[:, :, 0:Tn] for i in range(2)]
            qlf = bflat[:, 4096:6144].rearrange("p (a t) -> p a t", a=2)[:, :, 0:Tn]
            qlb = bflat[:, 6144:8192].rearrange("p (a t) -> p a t", a=2)[:, :, 0:Tn]
            Gb = bflat[:, 8192:10240].rearrange("p (d c e) -> p d c e", d=2, c=2)
            gat = [bflat[:, 10240 + i * 512:10240 + (i + 1) * 512] for i in range(2)]
            gT1 = bflat[:, 11264:15360].rearrange("p (f t) -> p f t", f=4)[:, :, 0:Tn]
            gT = [gT1, gT1]
            cview["posL1"] = posL
            sem_tb2 = T.new_dma_sem()
            T.dma("sp", lambda e: [e.dma_start(out=posL, in_=cst_d[:, COFF["posL1"]:COFF["posL1"] + 1024])],
                  sem_tb2, writes=(mb("cst"),))
            B_olb = [Buf("olb0"), Buf("olb1")]
            B_qTb = [Buf("qTb0"), Buf("qTb1")]
            B_gat = [Buf("gat0"), Buf("gat1")]
            _bgt = Buf("gT")
            B_gT = [_bgt, _bgt]
            B_ot = [Buf("ot0"), Buf("ot1")]
            B_sgt = [Buf("sgt0"), Buf("sgt1")]
            sem_l = [T.new_dma_sem() for _ in range(2)]
            sem_g = T.new_dma_sem()
            sem_gt = [T.new_dma_sem() for _ in range(2)]

            sem_g2 = [T.new_dma_sem() for _ in range(2)]

            def load_head(h):
                par = h % 2
                if cfg.exchange:
                    r_f = 0 * 512 + 0 * 256
                    r_b = 1 * 512 + 1 * 256
                    T.dma("sp", lambda e: [
                        e.dma_start(out=Gt2[par][:, 0, :, :], in_=exout[h][r_f:r_f + 256, :].rearrange("(c p) e -> p c e", p=128)),
                        e.dma_start(out=Gt2[par][:, 1, :, :], in_=exout[h][r_b:r_b + 256, :].rearrange("(c p) e -> p c e", p=128))],
                        sem_g2[par], reads=(mb("exout%d" % h),), writes=(mb("Gt%d" % par),), n=2)
                fns = [lambda e, h=h, par=par: e.dma_start(out=olb[par][:, 0:NT, :], in_=olsp[h, :, 0:NT * DV].rearrange("p (n e) -> p n e", e=DV))]
                wr = [B_olb[par]]
                if cfg.exchange:
                    fns.append(lambda e, h=h, par=par: e.dma_start(out=qTb[par], in_=qsp[h, :, 0:2 * Tn].rearrange("p (a t) -> p a t", a=2)))
                    wr.append(B_qTb[par])
                T.dma("sp", lambda e, fns=fns: [f(e) for f in fns], sem_l[par],
                      reads=(mb("olsp%d" % h), mb("qsp%d" % h)), writes=tuple(wr), n=len(fns))

            load_head(0)
            srcsB = []
            for hh in range(NH):
                srcsB += [wtile(w_in_d, 0, 8192 + hh * 512), wtile(w_in_d, 0, 8192 + hh * 512 + 256)]
            wsB = WStream(srcsB, hold=2, lookahead=2)
            for h in range(NH):
                par = h % 2
                if h + 1 < NH:
                    load_head(h + 1)
                sg0 = wsB.get(2 * h)
                sg1 = wsB.get(2 * h + 1)
                if cfg.exchange:
                    Gt = Gt2[par]
                    T.op("dve", lambda e: e.tensor_scalar(Gb[:, 0, :, :], Gt[:, 0, :, :], cs("flags", 1, 1), None, ALU.mult),
                         reads=(mb("Gt%d" % par), mb("cst")), writes=(mb("Gb"),))
                    T.op("dve", lambda e: e.tensor_scalar(Gb[:, 1, :, :], Gt[:, 1, :, :], cs("flags", 1, 0), None, ALU.mult),
                         reads=(mb("Gt%d" % par), mb("cst")), writes=(mb("Gb"),))
                    T.op("act", lambda e, h=h: e.activation(tabf[:, 0:Tn], cs("posL1", Tn), AF.Exp, scale=LG[:, h:h + 1]),
                         reads=(mb("LG"), mb("cst")), writes=(mb("tabf2"),))
                    T.op("act", lambda e, h=h: e.activation(tabb[:, 0:Tn], cs("posL1", Tn), AF.Exp, scale=NLG[:, 8 + h:9 + h],
                                                            bias=LB1025[:, 8 + h:9 + h]),
                         reads=(mb("NLG"), mb("LB1025"), mb("cst")), writes=(mb("tabb2"),))
                    for dc in range(2):
                        T.op("dve", lambda e, dc=dc, par=par: e.tensor_tensor(qlf[:, dc, :], qTb[par][:, dc, :], tabf[:, 0:Tn], ALU.mult),
                             reads=(B_qTb[par], mb("tabf2")), writes=(mb("qlf"),))
                        T.op("dve", lambda e, dc=dc, par=par: e.tensor_tensor(qlb[:, dc, :], qTb[par][:, dc, :], tabb[:, 0:Tn], ALU.mult),
                             reads=(B_qTb[par], mb("tabb2")), writes=(mb("qlb"),))
                mvall = xflat[:, 13320:13320 + 2 * NT].rearrange("p (n t) -> p n t", t=2)
                pcs = {}
                if cfg.exchange:
                    for n in range(NT):
                        pc = next_ps()
                        pairs = []
                        for dc in range(2):
                            pairs.append((qlf[:, dc, n * CH:(n + 1) * CH], Gb[:, 0, dc, :]))
                        for dc in range(2):
                            pairs.append((qlb[:, dc, n * CH:(n + 1) * CH], Gb[:, 1, dc, :]))
                        mm_group(pc, psum[pc][:, :], pairs, reads=(mb("qlf"), mb("qlb"), mb("Gb")))
                        T.op("dve", lambda e: e.tensor_tensor(olb[par][:, n, :], psum[pc][:, :], olb[par][:, n, :], ALU.add),
                             reads=(B_ps[pc], B_olb[par]), writes=(B_olb[par],))
                for n in range(NT):
                    s2 = n % 2
                    stt_ = bst[:, 0:6] if s2 == 0 else small[:, 56:62]
                    T.op("dve", lambda e: e.bn_stats(stt_, olb[par][:, n, :]),
                         reads=(B_olb[par],), writes=(mb("bns%d" % s2),))
                    T.op("dve", lambda e: e.bn_aggr(mvall[:, n, :], stt_), reads=(mb("bns%d" % s2),), writes=(mb("mvall"),))
                T.op("dve", lambda e: e.tensor_scalar(mvall[:, :, 1:2], mvall[:, :, 1:2], LN_EPS, None, ALU.add),
                     reads=(mb("mvall"),), writes=(mb("mvall"),))
                T.op("act", lambda e: e.activation(mvall[:, :, 1:2], mvall[:, :, 1:2], AF.Sqrt),
                     reads=(mb("mvall"),), writes=(mb("mvall"),))
                T.op("dve", lambda e: e.reciprocal(mvall[:, :, 1:2], mvall[:, :, 1:2]),
                     reads=(mb("mvall"),), writes=(mb("mvall"),))

                def p2_s1(n):
                    s2 = n % 2
                    pg = next_ps()
                    for half, si in enumerate((sg0, sg1)):
                        wv = wview(si)
                        pairs = [(hmT[:, k, n * CH:(n + 1) * CH], wv[:, k, :]) for k in range(KC)]
                        mm_group(pg, psum[pg][:, half * 256:(half + 1) * 256], pairs, reads=(B_ring[si],) + tuple(B_hm))
                    T.op("act", lambda e: e.activation(sgt[s2][:, :], psum[pg][:, :], AF.Silu),
                         reads=(B_ps[pg],), writes=(B_sgt[s2],))
                    T.op("dve", lambda e: e.tensor_scalar(ot[s2][:, :], olb[par][:, n, :], mvall[:, n, 0:1], mvall[:, n, 1:2],
                                                          ALU.subtract, ALU.mult),
                         reads=(B_olb[par], mb("mvall")), writes=(B_ot[s2],))
                    T.op("dve", lambda e: e.tensor_tensor(gat[s2][:, :], ot[s2][:, :], sgt[s2][:, :], ALU.mult),
                         reads=(B_ot[s2], B_sgt[s2]), writes=(B_gat[s2],))

                def p2_s2(n):
                    s2 = n % 2
                    pt = next_ps()
                    pt16 = psum[pt][:, 0:256].bitcast(BF16)
                    for fc in range(4):
                        T.op("pe", lambda e: e.transpose(
                            pt16[:, fc * 128:(fc + 1) * 128], gat[s2][:, fc * 128:(fc + 1) * 128], identb[:, :]),
                            reads=(B_gat[s2], mb("identb")), writes=(B_ps[pt],), signal=(fc == 3))
                    T.op("act", lambda e: e.activation(
                        gT[par][:, :, n * CH:(n + 1) * CH], pt16[:, :].rearrange("p (f t) -> p f t", f=4), AF.Copy),
                        reads=(B_ps[pt],), writes=(B_gT[par],))

                p2_s1(0)
                for n in range(NT):
                    if n + 1 < NT:
                        p2_s1(n + 1)
                    p2_s2(n)
                T.dma("sp", lambda e, h=h, par=par: [e.dma_start(out=gtsp[h, :, 0:4 * Tn].rearrange("p (f t) -> p f t", f=4), in_=gT[par])],
                      sem_gt[par], reads=(B_gT[par],), writes=(mb("gtsp"),))
            T.barrier()
        checkpoint("retB")
        sem_c1 = T.new_dma_sem()
        T.dma("sp", lambda e: [e.dma_start(out=bufB[:, 4 * h:4 * h + 4, 0:Tn], in_=gtsp[h, :, 0:4 * Tn].rearrange("p (f t) -> p f t", f=4))
                               for h in range(4)] +
                              [e.dma_start(out=hmT[:, 4 * (h - 4):4 * (h - 4) + 4, 0:Tn], in_=gtsp[h, :, 0:4 * Tn].rearrange("p (f t) -> p f t", f=4))
                               for h in range(4, 8)] +
                              [e.dma_start(out=xT[:, :, 0:Tn], in_=xspill[:, 0:KC * Tn].rearrange("p (k t) -> p k t", k=KC))],
              sem_c1, reads=(mb("gtsp"), mb("xspill")), writes=tuple(B_bb) + tuple(B_hm) + tuple(B_x), n=9)
        mod_pump(96)
        prescale_x(cfg, None, None, 0)
        srcs = []
        for mbk in range(8):
            for kh in range(2):
                srcs.append(wtile(w_o_d, kh * 2048, mbk * 256))

        def in_tile(kk, t0, tn):
            return bufB[:, kk, t0:t0 + tn] if kk < 16 else hmT[:, kk - 16, t0:t0 + tn]
        dense_acc_into_x(cfg, srcs, in_tile, tuple(B_bb) + tuple(B_hm), lambda m: modT[:, 1, c, 32 + m:33 + m], nk=32)
        T.barrier()

    def run_pass(cfg):
        c = cfg.cond
        Tn = cfg.T
        sem_x = T.new_dma_sem()
        sem_y = T.new_dma_sem()
        xsrc = cfg.x_d.rearrange("(k p) t -> p k t", p=128)
        if cfg.halo:
            T.dma("sp", lambda e: [e.dma_start(out=xT[:, :, 0:Tn], in_=xsrc[:, :, HALO:HALO + Tn]),
                                   e.dma_start(out=halo[:, :, 0:HALO], in_=xsrc[:, :, 0:HALO]),
                                   e.dma_start(out=halo[:, :, HALO:2 * HALO], in_=xsrc[:, :, HALO + Tn:2 * HALO + Tn])],
                  sem_x, writes=tuple(B_x) + (mb("halo"),), n=3)
        else:
            T.dma("sp", lambda e: [e.dma_start(out=xT[:, :, 0:Tn], in_=xsrc[:, :, 0:Tn])], sem_x, writes=tuple(B_x))
        mod_apply(cfg, lambda k: mder[:, 0, c, k:k + 1], lambda k: modT[:, 0, c, k:k + 1])
        if cfg.halo:
            for k in range(KC):
                T.op("dve", lambda e, k=k: e.tensor_scalar(hmT[:, k, TS:TS + 2 * HALO], halo[:, k, :], mder[:, 0, c, k:k + 1],
                                                           modT[:, 0, c, k:k + 1], ALU.mult, ALU.add),
                     reads=(mb("halo"), mb("mder"), mb("modT")), writes=(mb("hmhalo"),))
        checkpoint("modapply")
        conv_mixer(cfg)
        checkpoint("conv")
        layer_norm(cfg, "ln1_g", "ln1_b", 0, (0, c))
        checkpoint("ln1")
        ffn(cfg, 0)
        checkpoint("ffn0")
        layer_norm(cfg, "ln2_g", "ln2_b", 0, None)
        checkpoint("l0")
        mod_pump(max(0, 64 - mp["n"]))
        sem_spill = T.new_dma_sem()
        T.dma("sp", lambda e: [e.dma_start(out=xspill[:, 0:KC * Tn].rearrange("p (k t) -> p k t", k=KC), in_=xT[:, :, 0:Tn])],
              sem_spill, reads=tuple(B_x), writes=(mb("xspill"),))
        mod_apply(cfg, lambda k: mder[:, 1, c, k:k + 1], lambda k: modT[:, 1, c, k:k + 1])
        retention(cfg)
        checkpoint("ret")
        layer_norm(cfg, "ln1_g", "ln1_b", 16, (1, c))
        ffn(cfg, 1)
        layer_norm(cfg, "ln2_g", "ln2_b", 16, None)
        checkpoint("l1")
        T.dma("sp", lambda e: [e.dma_start(out=cfg.y_d.rearrange("(k p) t -> p k t", p=128), in_=xT[:, :, 0:Tn])],
              sem_y, reads=tuple(B_x), writes=(mb("yout"),))
        T.barrier()

    cs_ = Cfg()
    cs_.T = TS
    cs_.segs = [(0, TS)]
    cs_.cond = 0
    cs_.halo = True
    cs_.rope = True
    cs_.init_state = True
    cs_.state_out = False
    cs_.exchange = True
    cs_.x_d = xs_d
    cs_.y_d = ys_d
    cp_ = Cfg()
    cp_.T = TP
    cp_.segs = [(0, 256), (256, 256)]
    cp_.cond = 1
    cp_.halo = False
    cp_.rope = False
    cp_.init_state = False
    cp_.state_out = True
    cp_.exchange = False
    cp_.x_d = xp_d
    cp_.y_d = yp_d

    try:
        if _stop_now:
            raise StopBuild()
        run_pass(cs_)
        checkpoint("pass_s")
        run_pass(cp_)
    except StopBuild:
        pass
    T.barrier()
    if stop is not None:
        sem_dbg = T.new_dma_sem()
        src = dbg(locals())
        T.dma("sp", lambda e: [e.dma_start(out=dbg_d[:, 0:src.shape[1]], in_=src)], sem_dbg)
        T.barrier()

    with nc.Block() as block:
        @block.tensor
        def _(e):
            T.replay("pe", e)

        @block.scalar
        def _(e):
            T.replay("act", e)

        @block.vector
        def _(e):
            T.replay("dve", e)

        @block.gpsimd
        def _(e):
            T.replay("pool", e)

        @block.sync
        def _(e):
            T.replay("sp", e)
    es.close()
    return nc


def _chunkT(v):
    v = np.asarray(v, np.float32).reshape(-1, 128)
    return np.ascontiguousarray(v.T)


def _consts(rank):
    cst = np.zeros((128, NCST), np.float32)
    p = np.arange(128)
    cst[:, COFF["ident"]:COFF["ident"] + 128] = np.eye(128, dtype=np.float32)
    perm = np.zeros((128, 128), np.float32)
    perm[p, (p + 64) % 128] = 1.0
    cst[:, COFF["perm"]:COFF["perm"] + 128] = perm
    dmat = (p[None, :] - p[:, None]).astype(np.float32)
    cst[:, COFF["Pm"]:COFF["Pm"] + 128] = np.maximum(dmat, 0)
    cst[:, COFF["Nm"]:COFF["Nm"] + 128] = np.maximum(-dmat, 0)
    cst[:, COFF["indF"]:COFF["indF"] + 128] = (dmat >= 0)
    cst[:, COFF["indB"]:COFF["indB"] + 128] = (dmat <= 0)
    t = np.arange(TS)
    cst[:, COFF["posC1"]:COFF["posC1"] + TS] = ((t % CH) + 1)[None, :]
    cst[:, COFF["posL1"]:COFF["posL1"] + TS] = (t + 1)[None, :]
    tg = rank * TS + t
    row = (tg // 64).astype(np.float32)
    col = (tg % 64).astype(np.float32)
    inv = (10000.0 ** (-(np.arange(64, dtype=np.float32)) / 64.0)).astype(np.float32)
    invp = inv[p % 64]
    sgn = np.where(p < 64, -1.0, 1.0).astype(np.float32)
    a0 = (row[None, :] * invp[:, None]).astype(np.float32)
    a1 = (col[None, :] * invp[:, None]).astype(np.float32)
    cst[:, COFF["cos0"]:COFF["cos0"] + TS] = np.cos(a0)
    cst[:, COFF["sin0"]:COFF["sin0"] + TS] = np.sin(a0) * sgn[:, None]
    cst[:, COFF["cos1"]:COFF["cos1"] + TS] = np.cos(a1)
    cst[:, COFF["sin1"]:COFF["sin1"] + TS] = np.sin(a1) * sgn[:, None]
    cst[:, COFF["pcol"]] = 127 - p
    cst[:, COFF["pcol"] + 1] = p
    cst[:, COFF["flags"]] = 1.0 if rank == 0 else 0.0
    cst[:, COFF["flags"] + 1] = 1.0 if rank == 1 else 0.0
    cst[:, COFF["ones"]:COFF["ones"] + 128] = 1.0
    return cst


_NC_CACHE = {}


def kernel(x_prompt, x_sample, state_ret, c, c_ctx, w_mod, b_mod, ln1_g, ln1_b, ln2_g, ln2_b,
           w_pw1, b_pw1, w_dw, b_dw, cn_g, cn_b, w_pw2, b_pw2,
           w_ret_in, ret_log2_rate, w_ret_o, w_ff1, b_ff1, w_ff2, b_ff2):
    if "nc" not in _NC_CACHE:
        _NC_CACHE["nc"] = build_program()
    nc = _NC_CACHE["nc"]
    in_maps = make_in_maps(x_prompt, x_sample, state_ret, c, c_ctx, w_mod, b_mod, ln1_g, ln1_b, ln2_g, ln2_b,
                           w_pw1, b_pw1, w_dw, b_dw, cn_g, cn_b, w_pw2, b_pw2,
                           w_ret_in, ret_log2_rate, w_ret_o, w_ff1, b_ff1, w_ff2, b_ff2)
    res = run_bass_kernel_spmd(nc, in_maps, core_ids=list(range(8)))
    return assemble(res.results)


def make_in_maps(x_prompt, x_sample, state_ret, c, c_ctx, w_mod, b_mod, ln1_g, ln1_b, ln2_g, ln2_b,
                 w_pw1, b_pw1, w_dw, b_dw, cn_g, cn_b, w_pw2, b_pw2,
                 w_ret_in, ret_log2_rate, w_ret_o, w_ff1, b_ff1, w_ff2, b_ff2):
    f = lambda a: np.ascontiguousarray(np.asarray(a, dtype=np.float32))
    x_prompt, x_sample, state_ret = f(x_prompt), f(x_sample), f(state_ret)
    vec = np.concatenate([
        _chunkT(f(b_mod)), _chunkT(f(ln1_g)), _chunkT(f(ln1_b)), _chunkT(f(ln2_g)), _chunkT(f(ln2_b)),
        _chunkT(f(b_pw1)), _chunkT(f(w_dw)), _chunkT(f(b_dw)), _chunkT(f(cn_g)), _chunkT(f(cn_b)),
        _chunkT(f(b_pw2)), _chunkT(f(b_ff1)), _chunkT(f(b_ff2))], axis=1)
    assert vec.shape == (128, NV)
    rate = np.ascontiguousarray(np.broadcast_to(f(ret_log2_rate).reshape(1, 16), (128, 16)))
    shared = dict(w_mod=f(w_mod), w_pw1=f(w_pw1)[0], w_pw2=f(w_pw2)[0], w_ret_in=f(w_ret_in)[0],
                  w_ret_o=f(w_ret_o)[0], w_ff1=f(w_ff1), w_ff2=f(w_ff2), vecT=vec, rate=rate)
    xs_pad = np.zeros((x_sample.shape[0], 2048 + 2 * HALO, D), np.float32)
    xs_pad[:, HALO:HALO + 2048] = x_sample
    in_maps = []
    for core in range(8):
        b, r = core // 2, core % 2
        xs = np.ascontiguousarray(xs_pad[b, r * TS:r * TS + TS + 2 * HALO].T)
        xp = np.ascontiguousarray(x_prompt[2 * core:2 * core + 2].reshape(TP, D).T)
        cond = np.stack([f(c)[b], f(c_ctx)], axis=0)
        condT = np.ascontiguousarray(cond.reshape(2, KC, 128).transpose(2, 1, 0).reshape(128, KC * 2))
        m = dict(shared)
        m.update(xs=xs, xp=xp, s0=np.ascontiguousarray(state_ret[b, 0]), condT=condT, cst=_consts(r))
        in_maps.append(m)
    return in_maps


def assemble(results):
    y_prompt = np.zeros((16, 256, D), np.float32)
    y_sample = np.zeros((4, 2048, D), np.float32)
    new_state = np.zeros((16, 1, 2, NH, DK, DV), np.float32)
    for core in range(8):
        b, r = core // 2, core % 2
        o = results[core]
        y_sample[b, r * TS:(r + 1) * TS] = o["ys"].T
        y_prompt[2 * core:2 * core + 2] = o["yp"].T.reshape(2, 256, D)
        new_state[2 * core:2 * core + 2, 0] = o["st"]
    return (y_prompt, y_sample, new_state)
```
